# Optimizing a Trainium2 kernel written in Bass

```python
import jax, jax.numpy as jnp
from jax import lax
import numpy as np

D_MODEL = 2048
BATCH = 2
SEQ = 8192
DEPTH = 1
DEC_BATCH = 128
DEC_SEQ = 8
PAST_LEN = 16384
PAGE_SIZE = 128

HEAD_DIM = 64
MIX_WIDTH = D_MODEL
N_HEADS_A = MIX_WIDTH // 2 // HEAD_DIM
N_KV_A = N_HEADS_A // 4
GROUP_A = N_HEADS_A // N_KV_A
N_HEADS_B = MIX_WIDTH // 2 // HEAD_DIM
WIN_A = 128
DIL_BRANCHES = ((128, 1), (512, 4), (2048, 16))
WIN_B = max(w for w, _ in DIL_BRANCHES)
BLK = 128
D_FF = 4 * D_MODEL
ROPE_THETA = 10000.0
LN_EPS = 1e-5
DN_ALPHA = (2.0 * DEPTH) ** 0.25
DN_BETA = (8.0 * DEPTH) ** -0.25
SCALE = HEAD_DIM ** -0.5
NEG_INF = -1e30
QA = N_HEADS_A * HEAD_DIM
KA = N_KV_A * HEAD_DIM
VA = KA
QB = N_HEADS_B * HEAD_DIM
KB = QB
VB = QB
IN_WIDTH = QA + KA + VA + QB + KB + VB

kernel_name = "hymba_swa_sink_dilated_deepnorm"


def layer_norm(x, g, b):
    xf = x.astype(jnp.float32)
    mu = jnp.mean(xf, -1, keepdims=True)
    var = jnp.mean(jnp.square(xf - mu), -1, keepdims=True)
    return ((xf - mu) * lax.rsqrt(var + LN_EPS) * g.astype(jnp.float32) + b.astype(jnp.float32)).astype(x.dtype)


def rope(x, pos):
    half = HEAD_DIM // 2
    inv = 1.0 / (ROPE_THETA ** (jnp.arange(half, dtype=jnp.float32) / half))
    ang = pos.astype(jnp.float32)[:, None] * inv[None, :]
    cos = jnp.cos(ang)[:, None, :]
    sin = jnp.sin(ang)[:, None, :]
    xf = x.astype(jnp.float32)
    x1, x2 = xf[..., :half], xf[..., half:]
    return jnp.concatenate([x1 * cos - x2 * sin, x2 * cos + x1 * sin], -1).astype(x.dtype)


def project(x, w_in, pos):
    B, T = x.shape[:2]
    h = jnp.einsum('btd,de->bte', x, w_in)
    cuts = [int(c) for c in np.cumsum([QA, KA, VA, QB, KB])]
    qa, ka, va, qb, kb, vb = jnp.split(h, cuts, axis=-1)
    qa = rope(qa.reshape(B, T, N_HEADS_A, HEAD_DIM), pos)
    ka = rope(ka.reshape(B, T, N_KV_A, HEAD_DIM), pos)
    va = va.reshape(B, T, N_KV_A, HEAD_DIM)
    qb = rope(qb.reshape(B, T, N_HEADS_B, HEAD_DIM), pos)
    kb = rope(kb.reshape(B, T, N_HEADS_B, HEAD_DIM), pos)
    vb = vb.reshape(B, T, N_HEADS_B, HEAD_DIM)
    return qa, ka, va, qb, kb, vb


def band_keys(x):
    X, L = x.shape[:2]
    nb = L // BLK
    xb = x.reshape((X, nb, BLK) + x.shape[2:])
    prev = jnp.concatenate([jnp.zeros_like(xb[:, :1]), xb[:, :-1]], axis=1)
    return jnp.concatenate([prev, xb], axis=2)


def band_pos(nb):
    return (jnp.arange(nb)[:, None] - 1) * BLK + jnp.arange(2 * BLK)[None, :]


def sink_attention(q, k, v, q_pos, k_pos, sinks):
    s = jnp.einsum('bntkgd,bnskd->bnkgts', q, k).astype(jnp.float32) * SCALE
    dist = q_pos[:, :, None] - k_pos[:, None, :]
    valid = (dist >= 0) & (dist < WIN_A) & (k_pos[:, None, :] >= 0)
    s = jnp.where(valid[None, :, None, None], s, NEG_INF)
    sink = sinks.astype(jnp.float32).reshape(N_KV_A, GROUP_A)[None, None, :, :, None, None]
    m = jnp.maximum(jnp.max(s, -1, keepdims=True), sink)
    p = jnp.exp(s - m)
    denom = jnp.sum(p, -1, keepdims=True) + jnp.exp(sink - m)
    return jnp.einsum('bnkgts,bnskd->bntkgd', (p / denom).astype(v.dtype), v)


def window_sink_prompt(qa, ka, va, sinks):
    B, L = qa.shape[:2]
    nb = L // BLK
    q = qa.reshape(B, nb, BLK, N_KV_A, GROUP_A, HEAD_DIM)
    o = sink_attention(q, band_keys(ka), band_keys(va), jnp.arange(L).reshape(nb, BLK), band_pos(nb), sinks)
    return o.reshape(B, L, QA)


def window_sink_sample(qa, ka, va, cache_k, cache_v, sinks):
    DB, T = qa.shape[:2]
    n_past = cache_k.shape[1]
    k_ext = jnp.concatenate([cache_k, ka], axis=1)
    v_ext = jnp.concatenate([cache_v, va], axis=1)
    q = qa.reshape(DB, 1, T, N_KV_A, GROUP_A, HEAD_DIM)
    q_pos = (PAST_LEN + jnp.arange(T))[None]
    k_pos = (PAST_LEN - n_past + jnp.arange(n_past + T))[None]
    o = sink_attention(q, k_ext[:, None], v_ext[:, None], q_pos, k_pos, sinks)
    return o.reshape(DB, T, QA), k_ext[:, -n_past:], v_ext[:, -n_past:]


def probs_lse(s, valid):
    s = jnp.where(valid, s, NEG_INF)
    m = jnp.max(s, -1, keepdims=True)
    e = jnp.exp(s - m)
    l = jnp.sum(e, -1, keepdims=True)
    return e / l, (m + jnp.log(l))[..., 0]


def dilated_branch_prompt(q, k, v, window, r):
    B, L, H, D = q.shape
    n_keys = window // r
    M = L // r
    Mp = -(-M // BLK) * BLK
    nb = Mp // BLK

    def sub(x):
        x = x.reshape(B, M, r, H, D).transpose(0, 2, 1, 3, 4)
        x = jnp.pad(x, ((0, 0), (0, 0), (0, Mp - M), (0, 0), (0, 0)))
        return x.reshape(B * r, Mp, H, D)

    qb = sub(q).reshape(B * r, nb, BLK, H, D)
    kb = band_keys(sub(k))
    vb = band_keys(sub(v))
    qp = jnp.arange(Mp).reshape(nb, BLK)
    kp = band_pos(nb)
    dist = qp[:, :, None] - kp[:, None, :]
    valid = (dist >= 0) & (dist <= n_keys) & (kp[:, None, :] >= 0)
    s = jnp.einsum('xnthd,xnshd->xnhts', qb, kb).astype(jnp.float32) * SCALE
    p, lse = probs_lse(s, valid[None, :, None])
    o = jnp.einsum('xnhts,xnshd->xnthd', p.astype(vb.dtype), vb)
    o = o.reshape(B, r, Mp, H, D)[:, :, :M].transpose(0, 2, 1, 3, 4).reshape(B, L, H, D)
    lse = lse.transpose(0, 1, 3, 2).reshape(B, r, Mp, H)[:, :, :M].transpose(0, 2, 1, 3).reshape(B, L, H)
    return o, lse


def dilated_branch_sample(q, k_ext, v_ext, n_past, window, r):
    T = q.shape[1]
    n_keys = window // r
    idx = n_past + jnp.arange(T)[:, None] - r * jnp.arange(n_keys + 1)[None, :]
    valid = idx >= 0
    idx = jnp.maximum(idx, 0)
    kg = k_ext[:, idx]
    vg = v_ext[:, idx]
    s = jnp.einsum('bthd,btjhd->bthj', q, kg).astype(jnp.float32) * SCALE
    p, lse = probs_lse(s, valid[None, :, None, :])
    o = jnp.einsum('bthj,btjhd->bthd', p.astype(vg.dtype), vg)
    return o, lse


def merge_branches(outs, lses):
    w = jax.nn.softmax(jnp.stack(lses, 0), axis=0)
    o = jnp.einsum('ibth,ibthd->bthd', w, jnp.stack(outs, 0).astype(jnp.float32))
    return o.astype(outs[0].dtype)


def dilated_prompt(qb, kb, vb):
    outs, lses = [], []
    for window, r in DIL_BRANCHES:
        o, l = dilated_branch_prompt(qb, kb, vb, window, r)
        outs.append(o)
        lses.append(l)
    B, L = qb.shape[:2]
    return merge_branches(outs, lses).reshape(B, L, QB)


def dilated_sample(qb, kb, vb, cache_k, cache_v):
    DB, T = qb.shape[:2]
    n_past = cache_k.shape[1]
    k_ext = jnp.concatenate([cache_k, kb], axis=1)
    v_ext = jnp.concatenate([cache_v, vb], axis=1)
    outs, lses = [], []
    for window, r in DIL_BRANCHES:
        o, l = dilated_branch_sample(qb, k_ext, v_ext, n_past, window, r)
        outs.append(o)
        lses.append(l)
    return merge_branches(outs, lses).reshape(DB, T, QB), k_ext[:, -n_past:], v_ext[:, -n_past:]


def post_block(x, a, b, w_out, ln1_g, ln1_b, w_up, w_down, ln2_g, ln2_b):
    mix = jnp.concatenate([a, b], axis=-1)
    h = layer_norm(DN_ALPHA * x + jnp.einsum('btm,md->btd', mix, w_out), ln1_g, ln1_b)
    f = jnp.einsum('btf,fd->btd', jnp.square(jax.nn.relu(jnp.einsum('btd,df->btf', h, w_up))), w_down)
    return layer_norm(DN_ALPHA * h + f, ln2_g, ln2_b)


def setup_inputs(seed: int = 0) -> dict:
    key = jax.random.key(seed)
    ks = jax.random.split(key, 16)
    L_A = min(WIN_A, PAST_LEN)
    L_B = min(WIN_B, PAST_LEN)
    f32 = jnp.float32
    col_scale = jnp.concatenate([
        jnp.ones((QA + KA,), f32), jnp.full((VA,), DN_BETA, f32),
        jnp.ones((QB + KB,), f32), jnp.full((VB,), DN_BETA, f32)])
    return {
        "x_prompt": jax.random.normal(ks[0], (BATCH, SEQ, D_MODEL), f32),
        "x_sample": jax.random.normal(ks[1], (DEC_BATCH, DEC_SEQ, D_MODEL), f32),
        "cache_a_k": jax.random.normal(ks[2], (DEPTH, DEC_BATCH, L_A, N_KV_A, HEAD_DIM), f32),
        "cache_a_v": jax.random.normal(ks[3], (DEPTH, DEC_BATCH, L_A, N_KV_A, HEAD_DIM), f32) * DN_BETA,
        "cache_b_k": jax.random.normal(ks[4], (DEPTH, DEC_BATCH, L_B, N_HEADS_B, HEAD_DIM), f32),
        "cache_b_v": jax.random.normal(ks[5], (DEPTH, DEC_BATCH, L_B, N_HEADS_B, HEAD_DIM), f32) * DN_BETA,
        "w_in": jax.random.normal(ks[6], (DEPTH, D_MODEL, IN_WIDTH), f32) * (D_MODEL ** -0.5) * col_scale,
        "sinks": jax.random.normal(ks[7], (DEPTH, N_HEADS_A), f32),
        "w_out": jax.random.normal(ks[8], (DEPTH, MIX_WIDTH, D_MODEL), f32) * (MIX_WIDTH ** -0.5) * DN_BETA,
        "ln1_g": 1.0 + 0.02 * jax.random.normal(ks[9], (DEPTH, D_MODEL), f32),
        "ln1_b": 0.02 * jax.random.normal(ks[10], (DEPTH, D_MODEL), f32),
        "w_up": jax.random.normal(ks[11], (DEPTH, D_MODEL, D_FF), f32) * (D_MODEL ** -0.5),
        "w_down": jax.random.normal(ks[12], (DEPTH, D_FF, D_MODEL), f32) * (D_FF ** -0.5) * DN_BETA,
        "ln2_g": 1.0 + 0.02 * jax.random.normal(ks[13], (DEPTH, D_MODEL), f32),
        "ln2_b": 0.02 * jax.random.normal(ks[14], (DEPTH, D_MODEL), f32),
    }


def reference(x_prompt, x_sample, cache_a_k, cache_a_v, cache_b_k, cache_b_v,
              w_in, sinks, w_out, ln1_g, ln1_b, w_up, w_down, ln2_g, ln2_b):
    L = x_prompt.shape[1]
    T = x_sample.shape[1]
    pos_p = jnp.arange(L)
    pos_s = PAST_LEN + jnp.arange(T)
    hp, hs = x_prompt, x_sample
    pak, pav, pbk, pbv, sak, sav, sbk, sbv = [], [], [], [], [], [], [], []
    for l in range(DEPTH):
        qa, ka, va, qb, kb, vb = project(hp, w_in[l], pos_p)
        a = window_sink_prompt(qa, ka, va, sinks[l])
        b = dilated_prompt(qb, kb, vb)
        hp = post_block(hp, a, b, w_out[l], ln1_g[l], ln1_b[l], w_up[l], w_down[l], ln2_g[l], ln2_b[l])
        pak.append(ka[:, -min(WIN_A, L):])
        pav.append(va[:, -min(WIN_A, L):])
        pbk.append(kb[:, -min(WIN_B, L):])
        pbv.append(vb[:, -min(WIN_B, L):])
        qa, ka, va, qb, kb, vb = project(hs, w_in[l], pos_s)
        a, nak, nav = window_sink_sample(qa, ka, va, cache_a_k[l], cache_a_v[l], sinks[l])
        b, nbk, nbv = dilated_sample(qb, kb, vb, cache_b_k[l], cache_b_v[l])
        hs = post_block(hs, a, b, w_out[l], ln1_g[l], ln1_b[l], w_up[l], w_down[l], ln2_g[l], ln2_b[l])
        sak.append(nak)
        sav.append(nav)
        sbk.append(nbk)
        sbv.append(nbv)
    return (hp, hs, jnp.stack(pak), jnp.stack(pav), jnp.stack(pbk), jnp.stack(pbv),
            jnp.stack(sak), jnp.stack(sav), jnp.stack(sbk), jnp.stack(sbv))
```

```python
import numpy as np
from contextlib import ExitStack
import concourse.bass as bass
import concourse.mybir as mybir
from concourse.bass_utils import run_bass_kernel_spmd

F32 = mybir.dt.float32
BF16 = mybir.dt.bfloat16
AF = mybir.ActivationFunctionType
ALU = mybir.AluOpType

D = 2048
NCH = 16
DFF = 8192
NQT = 4
NWT = 20
NSTEP = 4
PAST = 16384
SCALE = 0.125
LN_EPS = 1e-5
ALPHA = 2.0 ** 0.25
NSEQ = 16
DEBUG_PARTS = None
STOP_AT = 99
DBG_PAIRS = None
DBG_CUT = 0


class Buf:
    __slots__ = ("w", "r", "dsem", "dcnt", "name", "excl")

    def __init__(self, name=""):
        self.excl = False
        self.w = None
        self.r = {}
        self.dsem = None
        self.dcnt = 0
        self.name = name


class TB:
    __slots__ = ("t", "b")

    def __init__(self, t, name=""):
        self.t = t
        self.b = Buf(name)


class Sch:
    def __init__(self, nc, es):
        self.nc = nc
        self.es = es
        self.eng = {"pe": nc.tensor, "act": nc.scalar, "dve": nc.vector, "pool": nc.gpsimd, "sp": nc.sync}
        self.sem = {}
        self.cnt = {}
        self.seen = {}
        self.nsem = 0
        self.dbufs = []
        self.dpool = []
        self.dpool_sw = []
        self.dkind = {}
        self.pe_mode = None
        self.swq = []
        self.allsems = []
        for k in self.eng:
            self.sem[k] = self.newsem("p_" + k)
            self.cnt[k] = 0
            self.seen[k] = {}

    def newsem(self, name):
        self.nsem += 1
        s = self.es.enter_context(self.nc.semaphore(f"{name}_{self.nsem}"))
        return s

    def _wait(self, e, toks):
        best = {}
        for (sem, val) in toks:
            k = id(sem)
            if k not in best or best[k][1] < val:
                best[k] = (sem, val)
        for k, (sem, val) in best.items():
            if e == "pe" and sem is self.sem["pe"]:
                continue
            if self.seen[e].get(k, 0) >= val:
                continue
            self.eng[e].wait_ge(sem, val)
            self.seen[e][k] = val

    def _deps(self, rd, wr, dma_buf=None):
        toks = []
        for b in rd:
            if b.w is not None:
                toks.append(b.w)
        for b in wr:
            if b.w is not None:
                if not (dma_buf is not None and b is dma_buf and b.w[0] is b.dsem):
                    toks.append(b.w)
            toks.extend(b.r.values())
        return toks

    def _mark(self, tok, rd, wr):
        for b in wr:
            b.w = tok
            b.r = {}
        for b in rd:
            k = id(tok[0])
            if k not in b.r or b.r[k][1] < tok[1]:
                b.r[k] = tok

    def op(self, e, fn, rd=(), wr=()):
        if self.cnt[e] >= 30000:
            self.sem[e] = self.newsem("p_" + e)
            self.cnt[e] = 0
        ex = [b for b in rd if b.excl]
        if ex:
            wr = list(wr) + ex
        self._wait(e, self._deps(rd, wr))
        inst = fn()
        self.cnt[e] += 1
        inst.then_inc(self.sem[e], 1)
        self._mark((self.sem[e], self.cnt[e]), rd, wr)

    def dma(self, q, out, in_, rd=(), wr=(), ndesc=64, **kw):
        pb = wr[0] if len(wr) else rd[0]
        if q == "pool":
            while self.swq and sum(n for _, n in self.swq) + ndesc > 260:
                sem0 = self.swq[0][0][0]
                same = [t for t, _ in self.swq if t[0] is sem0]
                self.swq = [(t, n) for t, n in self.swq if t[0] is not sem0]
                self._wait("pool", [(sem0, max(v for _, v in same))])
        kind = "sw" if q == "pool" else "hw"
        if pb.dsem is None:
            pool = self.dpool_sw if kind == "sw" else self.dpool
            if pool:
                pb.dsem, pb.dcnt = pool.pop()
            else:
                pb.dsem = self.newsem("d" + kind)
            self.dbufs.append(pb)
            self.dkind[id(pb)] = kind
        assert self.dkind[id(pb)] == kind, "mixed SW/HW DGE on one semaphore"
        self._wait(q, self._deps(rd, wr, dma_buf=pb))
        pb.dcnt += 16
        self.eng[q].dma_start(out=out, in_=in_, **kw).then_inc(pb.dsem, 16)
        self._mark((pb.dsem, pb.dcnt), rd, wr)
        if q == "pool":
            self.swq.append(((pb.dsem, pb.dcnt), ndesc))

    def barrier(self):
        toks = [(self.sem[k], self.cnt[k]) for k in self.eng if self.cnt[k] > 0]
        toks += [(b.dsem, b.dcnt) for b in self.dbufs]
        for e in self.eng:
            self._wait(e, [t for t in toks if t[0] is not self.sem[e]])
        for b in self.dbufs:
            (self.dpool_sw if self.dkind[id(b)] == "sw" else self.dpool).append((b.dsem, b.dcnt))
            b.dsem = None
        self.dbufs = []
        self.dkind = {}
        self.swq = []

    def finish(self):
        toks = [(b.dsem, b.dcnt) for b in self.dbufs] + list(self.dpool) + list(self.dpool_sw)
        toks += [(self.sem[k], self.cnt[k]) for k in self.eng if self.cnt[k] > 0 and k != "sp"]
        self._wait("sp", toks)


def build_nc():
    nc = bass.Bass("TRN2", target_bir_lowering=False)

    def din(name, shape):
        return nc.dram_tensor(name, shape, F32, kind="ExternalInput").ap()

    def dout(name, shape):
        return nc.dram_tensor(name, shape, F32, kind="ExternalOutput").ap()

    xh = din("xh", [4096, D])
    xs = din("xs", [128, D])
    cak = din("cak", [NSEQ, 128, 256])
    cav = din("cav", [NSEQ, 128, 256])
    full = DEBUG_PARTS is None or NSTEP in DEBUG_PARTS
    nsq_d = NSEQ if full else 1
    cbk = din("cbk", [nsq_d, 2048, 1024])
    cbv = din("cbv", [nsq_d, 2048, 1024])
    w_in = din("w_in", [D, 4608])
    w_out = din("w_out", [D, D])
    w_up = din("w_up", [D, DFF])
    w_down = din("w_down", [DFF, D])
    sinks = din("sinks", [1, 16])
    lnp_d = din("lnp", [128, 4 * 16])
    c2_d = din("c2", [128, 33 * 64])
    s2_d = din("s2", [128, 33 * 64])
    valid_d = din("valid", [128, 4 * NWT + 1])
    mk_d = din("mk", [128, 8 * 128])
    msb_d = din("msb", [128, 16 * 8])
    mnb_d = din("mnb", [128, 16 * 8])
    msa_d = din("msa", [128, 8])
    mna_d = din("mna", [128, 16 * 8])
    identf_d = din("identf", [128, 128])
    alphai_d = din("alphai", [128, 128])

    y = dout("y", [2048, D])
    ys = dout("ys", [128, D])
    pak = dout("pak", [128, 256])
    pav = dout("pav", [128, 256])
    pbk = dout("pbk", [2048, 1024])
    pbv = dout("pbv", [2048, 1024])
    sak = dout("sak", [NSEQ, 128, 256])
    sav = dout("sav", [NSEQ, 128, 256])
    sbk = dout("sbk", [nsq_d, 2048, 1024])
    sbv = dout("sbv", [nsq_d, 2048, 1024])

    w_in_r = w_in.rearrange("(c p) n -> p c n", p=128)
    w_out_r = w_out.rearrange("(c p) n -> p c n", p=128)
    w_up_r = w_up.rearrange("(c p) n -> p c n", p=128)
    w_down_r = w_down.rearrange("(c p) n -> p c n", p=128)

    with ExitStack() as es:
        S = Sch(nc, es)
        V = nc.vector
        A = nc.scalar
        G = nc.gpsimd
        PE = nc.tensor

        uid = {"n": 0}

        def sb(stack, name, shape, dt=F32):
            uid["n"] += 1
            return TB(stack.enter_context(nc.sbuf_tensor(f"s{uid['n']}_{name}", shape, dt)), name)

        identf = sb(es, "identf", [128, 128])
        alphai = sb(es, "alphai", [128, 128])
        onesf = sb(es, "onesf", [128, 128])
        onesb = sb(es, "onesb", [128, 128], BF16)
        mk = sb(es, "mk", [128, 8, 128], BF16)
        msb = sb(es, "msb", [128, 16, 8], BF16)
        mnb = sb(es, "mnb", [128, 16, 8], BF16)
        msa = sb(es, "msa", [128, 8], BF16)
        mna = sb(es, "mna", [128, 16, 8], BF16)
        esink = sb(es, "esink", [128, 32])
        lnp = sb(es, "lnp", [128, 4, 16])
        lnpa = sb(es, "lnpa", [128, 2, 16])
        c2 = sb(es, "c2", [128, 33, 64])
        s2 = sb(es, "s2", [128, 33, 64])
        valid = sb(es, "valid", [128, 4 * NWT + 1])
        mixT = sb(es, "mixT", [128, 16, 512], BF16)
        banks = [TB(es.enter_context(nc.psum_tensor(f"bank{i}", [128, 512], F32)), f"bank{i}") for i in range(8)]
        for bk in banks:
            bk.b.excl = True
        PJ, TR, SC, AC = banks[0:2], banks[2:4], banks[4:6], banks[6:8]
        SCSETS = [(SC[0], SC[1]), (PJ[0], PJ[1]), (TR[0], TR[1])]

        S.dma("sp", identf.t[:, :], identf_d[:, :], wr=[identf.b])
        S.dma("sp", alphai.t[:, :], alphai_d[:, :], wr=[alphai.b])
        S.dma("sp", lnp.t[:, :, :], lnp_d.rearrange("p (k d) -> p k d", k=4), wr=[lnp.b])
        S.dma("sp", c2.t[:, :, :], c2_d.rearrange("p (w e) -> p w e", e=64), wr=[c2.b])
        S.dma("sp", s2.t[:, :, :], s2_d.rearrange("p (w e) -> p w e", e=64), wr=[s2.b])
        S.dma("sp", valid.t[:, :], valid_d[:, :], wr=[valid.b])
        S.dma("sp", esink.t[:, 0:16], sinks.partition_broadcast(128), wr=[esink.b])
        S.dma("pool", mk.t[:, :, :], mk_d.rearrange("p (m q) -> p m q", m=8), wr=[mk.b])
        S.dma("pool", msb.t[:, :, :], msb_d.rearrange("p (k t) -> p k t", t=8), wr=[msb.b])
        S.dma("pool", mnb.t[:, :, :], mnb_d.rearrange("p (k t) -> p k t", t=8), wr=[mnb.b])
        S.dma("pool", msa.t[:, :], msa_d[:, :], wr=[msa.b])
        S.dma("pool", mna.t[:, :, :], mna_d.rearrange("p (k t) -> p k t", t=8), wr=[mna.b])
        S.op("dve", lambda: V.memset(onesf.t[:, :], 1.0), wr=[onesf.b])
        S.op("dve", lambda: V.memset(onesb.t[:, :], 1.0), wr=[onesb.b])
        S.op("act", lambda: A.activation(out=esink.t[:, 0:16], in_=esink.t[:, 0:16], func=AF.Exp),
             rd=[esink.b], wr=[esink.b])
        S.op("dve", lambda: V.memset(esink.t[:, 16:32], 0.0), rd=[esink.b], wr=[esink.b])
        S.op("dve", lambda: V.tensor_scalar(out=lnpa.t[:, 0, :], in0=lnp.t[:, 0, :], scalar1=ALPHA, scalar2=None,
                                            op0=ALU.mult), rd=[lnp.b], wr=[lnpa.b])
        S.op("dve", lambda: V.tensor_scalar(out=lnpa.t[:, 1, :], in0=lnp.t[:, 1, :], scalar1=ALPHA, scalar2=None,
                                            op0=ALU.mult), rd=[lnp.b], wr=[lnpa.b])

        rr = {"pj": 0, "tr": 0, "sc": 0, "ac": 0}

        def nxt(key, lst):
            i = rr[key]
            rr[key] = i + 1
            return lst[i % len(lst)]

        def pe_mode(m):
            if S.pe_mode is not None and S.pe_mode != m:
                PE.drain()
            S.pe_mode = m

        def mm(out, lhsT, rhs, start, stop, rd, wr):
            pe_mode("mm")
            S.op("pe", lambda: PE.matmul(out, lhsT=lhsT, rhs=rhs, start=start, stop=stop, skip_group_check=True),
                 rd=rd, wr=wr)

        def tp(out, in_, rd, wr):
            pe_mode("tp")
            S.op("pe", lambda: PE.transpose(out, in_, identf.t[:, :]), rd=list(rd) + [identf.b], wr=wr)

        def emit_step(step):
            sample = step == NSTEP
            nwt = 1 if sample else NWT
            nqt = 1 if sample else NQT
            T = nqt * 128
            tb0 = 32 if sample else 4 * step
            vb0 = 4 * NWT if sample else NWT * step

            with ExitStack() as pa:
                xT = sb(pa, "xT", [128, NCH, nwt * 128], BF16)
                xTb = [Buf() for _ in range(nwt)]
                xst = [sb(pa, f"xst{i}", [128, D]) for i in range(2)]
                slabs = [sb(pa, f"slabA{i}", [128, NCH, 384], BF16) for i in range(2)]
                pst = [sb(pa, f"pst{i}", [128, 384]) for i in range(2)]
                t1 = [sb(pa, f"t1_{i}", [128, 256]) for i in range(2)]
                t2 = [sb(pa, f"t2_{i}", [128, 256]) for i in range(2)]
                kr = [sb(pa, f"kr{i}", [128, 256]) for i in range(2)]
                Pb = [sb(pa, f"P{i}", [128, 2, 512], BF16) for i in range(3)]
                d2 = [sb(pa, f"d2_{i}", [128, 256]) for i in range(2)]
                rdn = [sb(pa, f"rdn{i}", [128, 256]) for i in range(2)]
                if sample:
                    npr = 16
                    KT = sb(pa, "KT", [128, npr, 128], BF16)
                    QT = sb(pa, "QT", [128, npr, 128], BF16)
                    Vb = sb(pa, "Vb", [128, npr, 128], BF16)
                    KTb = [[Buf()] for _ in range(npr)]
                    QTb = [[Buf()] for _ in range(npr)]
                    Vbb = [[Buf()] for _ in range(npr)]
                else:
                    KT = sb(pa, "KT", [128, nwt * 128], BF16)
                    QT = sb(pa, "QT", [128, nqt * 128], BF16)
                    Vb = sb(pa, "Vb", [128, nwt, 128], BF16)
                    KTb = [Buf() for _ in range(nwt)]
                    QTb = [Buf() for _ in range(nqt)]
                    Vbb = [Buf() for _ in range(nwt)]

                for wt in range(nwt):
                    st = xst[wt % 2]
                    src = xs[:, :] if sample else xh[(4 * step + wt) * 128:(4 * step + wt + 1) * 128, :]
                    S.dma("sp", st.t[:, :], src, wr=[st.b])
                    for g in range(4):
                        bank = nxt("tr", TR)
                        for i in range(4):
                            c = 4 * g + i
                            tp(bank.t[:, i * 128:(i + 1) * 128], st.t[:, c * 128:(c + 1) * 128], [st.b], [bank.b])
                        dst = xT.t[:, 4 * g:4 * g + 4, wt * 128:(wt + 1) * 128]
                        srcp = bank.t[:, :].rearrange("p (a b) -> p a b", a=4)
                        if g % 2 == 0:
                            S.op("act", lambda: A.activation(out=dst, in_=srcp, func=AF.Copy), rd=[bank.b], wr=[xTb[wt]])
                        else:
                            S.op("dve", lambda: V.tensor_copy(out=dst, in_=srcp), rd=[bank.b], wr=[xTb[wt]])

                cnt = {"i": 0, "p": 0}
                for pr in (DBG_PAIRS if DBG_PAIRS is not None else range(16 if STOP_AT > 1 else 0)):
                    isA = pr < 8
                    slab = slabs[pr % 2]
                    if isA:
                        kv = pr // 2
                        qc = pr * 128
                        S.dma("pool", slab.t[:, :, 0:128], w_in_r[:, :, qc:qc + 128], wr=[slab.b], ndesc=128)
                        S.dma("pool", slab.t[:, :, 128:192], w_in_r[:, :, 1024 + 64 * kv:1088 + 64 * kv], wr=[slab.b],
                              ndesc=128)
                        S.dma("pool", slab.t[:, :, 256:320], w_in_r[:, :, 1280 + 64 * kv:1344 + 64 * kv], wr=[slab.b],
                              ndesc=128)
                        S.op("dve", lambda: V.tensor_copy(out=slab.t[:, :, 192:256], in_=slab.t[:, :, 128:192]),
                             rd=[slab.b], wr=[slab.b])
                        S.op("dve", lambda: V.tensor_copy(out=slab.t[:, :, 320:384], in_=slab.t[:, :, 256:320]),
                             rd=[slab.b], wr=[slab.b])
                    else:
                        hp = pr - 8
                        for s3 in range(3):
                            cc = 1536 + 1024 * s3 + hp * 128
                            S.dma("pool", slab.t[:, :, 128 * s3:128 * s3 + 128], w_in_r[:, :, cc:cc + 128], wr=[slab.b],
                                  ndesc=128)
                    wts = range(nwt) if (not isA or sample) else range(nwt - nqt - 1, nwt)
                    for wt in wts:
                        isq = wt >= nwt - nqt
                        qi = wt - (nwt - nqt)
                        c0 = 0 if isq else 128
                        bank = nxt("pj", PJ)
                        for c in range(NCH):
                            mm(bank.t[:, c0:384], xT.t[:, c, wt * 128:(wt + 1) * 128], slab.t[:, c, c0:384],
                               c == 0, c == NCH - 1, [xTb[wt], slab.b], [bank.b])
                        k = cnt["i"] % 2
                        cnt["i"] += 1
                        ps, a1, a2, ak = pst[k], t1[k], t2[k], kr[k]
                        S.op("act", lambda: A.activation(out=ps.t[:, c0:384], in_=bank.t[:, c0:384], func=AF.Copy),
                             rd=[bank.b], wr=[ps.b])
                        if DBG_CUT == 1:
                            continue
                        nh = (256 - c0) // 64
                        src3 = ps.t[:, c0:256].rearrange("p (h e) -> p h e", e=64)
                        tw = tb0 + wt
                        c2b = c2.t[:, tw, :].unsqueeze(1).broadcast_to([128, nh, 64])
                        s2lo = s2.t[:, tw, 0:32].unsqueeze(1).broadcast_to([128, nh, 32])
                        s2hi = s2.t[:, tw, 32:64].unsqueeze(1).broadcast_to([128, nh, 32])
                        a13 = a1.t[:, c0:256].rearrange("p (h e) -> p h e", e=64)
                        a23 = a2.t[:, c0:256].rearrange("p (h e) -> p h e", e=64)
                        S.op("dve", lambda: V.tensor_tensor(out=a13, in0=src3, in1=c2b, op=ALU.mult),
                             rd=[ps.b, c2.b], wr=[a1.b])
                        S.op("dve", lambda: V.tensor_tensor(out=a23[:, :, 0:32], in0=src3[:, :, 32:64], in1=s2lo,
                                                            op=ALU.mult), rd=[ps.b, s2.b], wr=[a2.b])
                        S.op("dve", lambda: V.tensor_tensor(out=a23[:, :, 32:64], in0=src3[:, :, 0:32], in1=s2hi,
                                                            op=ALU.mult), rd=[ps.b, s2.b], wr=[a2.b])
                        S.op("dve", lambda: V.tensor_tensor(out=ak.t[:, c0:256], in0=a1.t[:, c0:256],
                                                            in1=a2.t[:, c0:256], op=ALU.add),
                             rd=[a1.b, a2.b], wr=[ak.b])
                        if DBG_CUT == 3:
                            continue
                        bank2 = nxt("tr", TR)
                        tp(bank2.t[:, 0:128], ak.t[:, 128:256], [ak.b], [bank2.b])
                        if isq:
                            tp(bank2.t[:, 128:256], ak.t[:, 0:128], [ak.b], [bank2.b])
                        if DBG_CUT == 4:
                            continue
                        if sample:
                            kdst, kb_ = KT.t[:, pr, :], KTb[pr][0]
                            qdst, qb_ = QT.t[:, pr, :], QTb[pr][0]
                            vdst, vb_ = Vb.t[:, pr, :], Vbb[pr][0]
                        else:
                            kdst, kb_ = KT.t[:, wt * 128:(wt + 1) * 128], KTb[wt]
                            vdst, vb_ = Vb.t[:, wt, :], Vbb[wt]
                            if isq:
                                qdst, qb_ = QT.t[:, qi * 128:(qi + 1) * 128], QTb[qi]
                        S.op("act", lambda: A.activation(out=kdst, in_=bank2.t[:, 0:128], func=AF.Copy),
                             rd=[bank2.b], wr=[kb_])
                        if DBG_CUT == 5:
                            continue
                        if isq:
                            S.op("dve", lambda: V.tensor_copy(out=qdst, in_=bank2.t[:, 128:256]),
                                 rd=[bank2.b], wr=[qb_])
                        if DBG_CUT == 6:
                            continue
                        S.op("act", lambda: A.activation(out=vdst, in_=ps.t[:, 256:384], func=AF.Copy), rd=[ps.b], wr=[vb_])
                        if isq and not sample and DBG_CUT != 2:
                            r0 = (step * NQT + qi) * 128
                            if not isA:
                                hp = pr - 8
                                S.dma("sp", pbk[r0:r0 + 128, hp * 128:(hp + 1) * 128], ak.t[:, 128:256], rd=[ak.b])
                                S.dma("sp", pbv[r0:r0 + 128, hp * 128:(hp + 1) * 128], ps.t[:, 256:384], rd=[ps.b])
                            elif pr % 2 == 0 and step == NSTEP - 1 and qi == NQT - 1:
                                kv = pr // 2
                                S.dma("sp", pak[:, kv * 64:(kv + 1) * 64], ak.t[:, 128:192], rd=[ak.b])
                                S.dma("sp", pav[:, kv * 64:(kv + 1) * 64], ps.t[:, 256:320], rd=[ps.b])
                        if sample:
                            for sq in range(NSEQ):
                                if not isA:
                                    hp = pr - 8
                                    S.dma("sp", sbk[sq, 2040:2048, hp * 128:(hp + 1) * 128],
                                          ak.t[sq * 8:(sq + 1) * 8, 128:256], rd=[ak.b])
                                    S.dma("sp", sbv[sq, 2040:2048, hp * 128:(hp + 1) * 128],
                                          ps.t[sq * 8:(sq + 1) * 8, 256:384], rd=[ps.b])
                                elif pr % 2 == 0:
                                    kv = pr // 2
                                    S.dma("sp", sak[sq, 120:128, kv * 64:(kv + 1) * 64],
                                          ak.t[sq * 8:(sq + 1) * 8, 128:192], rd=[ak.b])
                                    S.dma("sp", sav[sq, 120:128, kv * 64:(kv + 1) * 64],
                                          ps.t[sq * 8:(sq + 1) * 8, 256:320], rd=[ps.b])

                    if sample or STOP_AT < 3:
                        continue
                    for qi in range(nqt):
                        wq = nwt - nqt + qi
                        if isA:
                            kts = [(wq, 0), (wq - 1, 1)]
                        else:
                            def mt_of(dl):
                                return 2 if dl == 0 else 3 if dl == 1 else 4 if dl in (2, 3) else 5 if dl == 4 else \
                                    7 if dl == 16 else 6
                            kts = [(wq - dl, mt_of(dl)) for dl in range(17)]
                        acc = nxt("ac", AC)
                        first = True
                        for c0_ in range(0, len(kts), 4):
                            chunk = kts[c0_:c0_ + 4]
                            scs = nxt("sc", SCSETS)
                            for a_, (kt, mt) in enumerate(chunk):
                                for h in range(2):
                                    mm(scs[h].t[:, a_ * 128:(a_ + 1) * 128],
                                       KT.t[64 * h:64 * h + 64, kt * 128:(kt + 1) * 128],
                                       QT.t[64 * h:64 * h + 64, qi * 128:(qi + 1) * 128], True, True,
                                       [KTb[kt], QTb[qi]], [scs[h].b])
                            n = len(chunk) * 128
                            P = Pb[cnt["p"] % 3]
                            cnt["p"] += 1
                            for h in range(2):
                                S.op("act", lambda: A.activation(out=P.t[:, h, 0:n], in_=scs[h].t[:, 0:n], func=AF.Exp,
                                                                 scale=SCALE), rd=[scs[h].b], wr=[P.b])
                            for a_, (kt, mt) in enumerate(chunk):
                                pv = P.t[:, :, a_ * 128:(a_ + 1) * 128]
                                mkb = mk.t[:, mt, :].unsqueeze(1).broadcast_to([128, 2, 128])
                                S.op("dve", lambda: V.scalar_tensor_tensor(
                                    out=pv, in0=pv, scalar=valid.t[:, vb0 + kt:vb0 + kt + 1], in1=mkb,
                                    op0=ALU.mult, op1=ALU.mult), rd=[P.b, valid.b, mk.b], wr=[P.b])
                            for a_, (kt, mt) in enumerate(chunk):
                                last = (c0_ + a_ == len(kts) - 1)
                                mm(acc.t[:, 0:128], Vb.t[:, kt, :], P.t[:, 0, a_ * 128:(a_ + 1) * 128], first, False,
                                   [Vbb[kt], P.b], [acc.b])
                                first = False
                                mm(acc.t[:, 128:256], Vb.t[:, kt, :], P.t[:, 1, a_ * 128:(a_ + 1) * 128], False, False,
                                   [Vbb[kt], P.b], [acc.b])
                                mm(acc.t[:, 256:512], onesb.t[:, :], P.t[:, :, a_ * 128:(a_ + 1) * 128], False, last,
                                   [onesb.b, P.b], [acc.b])
                        k = cnt["i"] % 2
                        cnt["i"] += 1
                        dd, rdd = d2[k], rdn[k]
                        esb = esink.t[:, 2 * pr:2 * pr + 2].unsqueeze(2).broadcast_to([128, 2, 128])
                        S.op("dve", lambda: V.tensor_tensor(out=dd.t[:, :].rearrange("p (h q) -> p h q", h=2),
                                                            in0=acc.t[:, 256:512].rearrange("p (h q) -> p h q", h=2),
                                                            in1=esb, op=ALU.add), rd=[acc.b, esink.b], wr=[dd.b])
                        S.op("dve", lambda: V.reciprocal(out=rdd.t[:, :], in_=dd.t[:, :]), rd=[dd.b], wr=[rdd.b])
                        S.op("dve", lambda: V.tensor_tensor(out=mixT.t[0:64, pr, qi * 128:(qi + 1) * 128],
                                                            in0=acc.t[0:64, 0:128], in1=rdd.t[0:64, 0:128], op=ALU.mult),
                             rd=[acc.b, rdd.b], wr=[mixT.b])
                        S.op("dve", lambda: V.tensor_tensor(out=mixT.t[64:128, pr, qi * 128:(qi + 1) * 128],
                                                            in0=acc.t[64:128, 128:256], in1=rdd.t[64:128, 128:256],
                                                            op=ALU.mult), rd=[acc.b, rdd.b], wr=[mixT.b])

                if sample:
                    kst = [sb(pa, f"kst{i}", [128, 1024]) for i in range(2)]
                    vst = [sb(pa, f"vst{i}", [128, 1024]) for i in range(2)]
                    vbf = [sb(pa, f"vbf{i}", [128, 1024], BF16) for i in range(2)]
                    KTc = [sb(pa, f"KTc{i}", [128, 8, 128], BF16) for i in range(2)]
                    kat = [sb(pa, f"kat{i}", [128, 256]) for i in range(2)]
                    vat = [sb(pa, f"vat{i}", [128, 256]) for i in range(2)]
                    kdup = [sb(pa, f"kdup{i}", [128, 4, 2, 64]) for i in range(2)]
                    vdup = [sb(pa, f"vdup{i}", [128, 4, 2, 64], BF16) for i in range(2)]
                    KTA = [sb(pa, f"KTA{i}", [128, 4, 128], BF16) for i in range(2)]
                    d2s = [sb(pa, f"d2s{i}", [128, 256]) for i in range(2)]
                    rds = [sb(pa, f"rds{i}", [128, 256]) for i in range(2)]
                    tcnt = {"t": 0}
                    for sq in range(NSEQ):
                        acc = nxt("ac", AC)
                        state = {"first": True}

                        def attend(pairs, ktf, ktbufs, vf, vbufs, mask_ap, mask_b, last):
                            scs = nxt("sc", SCSETS)
                            npairs = len(pairs)
                            for jj, pr_ in enumerate(pairs):
                                for h in range(2):
                                    mm(scs[h].t[:, jj * 8:(jj + 1) * 8], ktf(jj, h),
                                       QT.t[64 * h:64 * h + 64, pr_, sq * 8:(sq + 1) * 8], True, True,
                                       list(ktbufs(jj)) + [QTb[pr_][0]], [scs[h].b])
                            n = npairs * 8
                            P = Pb[cnt["p"] % 3]
                            cnt["p"] += 1
                            for h in range(2):
                                S.op("act", lambda: A.activation(out=P.t[:, h, 0:n], in_=scs[h].t[:, 0:n], func=AF.Exp,
                                                                 scale=SCALE), rd=[scs[h].b], wr=[P.b])
                            pv = P.t[:, :, 0:n].rearrange("p h (j t) -> p h j t", t=8)
                            mb = mask_ap.unsqueeze(1).unsqueeze(1).broadcast_to([128, 2, npairs, 8])
                            S.op("dve", lambda: V.tensor_tensor(out=pv, in0=pv, in1=mb, op=ALU.mult),
                                 rd=[P.b, mask_b], wr=[P.b])
                            for jj, pr_ in enumerate(pairs):
                                lst = last and jj == npairs - 1
                                mm(acc.t[:, pr_ * 32:pr_ * 32 + 8], vf(jj), P.t[:, 0, 8 * jj:8 * jj + 8],
                                   state["first"], False, list(vbufs(jj)) + [P.b], [acc.b])
                                state["first"] = False
                                mm(acc.t[:, pr_ * 32 + 8:pr_ * 32 + 16], vf(jj), P.t[:, 1, 8 * jj:8 * jj + 8],
                                   False, False, list(vbufs(jj)) + [P.b], [acc.b])
                                mm(acc.t[:, pr_ * 32 + 16:pr_ * 32 + 32], onesb.t[:, :], P.t[:, :, 8 * jj:8 * jj + 8],
                                   False, lst, [onesb.b, P.b], [acc.b])

                        bpairs = list(range(8, 16))
                        apairs = list(range(0, 8))
                        for kt in range(16):
                            k = tcnt["t"] % 2
                            tcnt["t"] += 1
                            ks, vs, vbb, ktc = kst[k], vst[k], vbf[k], KTc[k]
                            S.dma("sp", ks.t[:, :], cbk[sq, kt * 128:(kt + 1) * 128, :], wr=[ks.b])
                            S.dma("act", vs.t[:, :], cbv[sq, kt * 128:(kt + 1) * 128, :], wr=[vs.b])
                            if kt == 0:
                                S.dma("sp", sbk[sq, 0:120, :], ks.t[8:128, :], rd=[ks.b])
                                S.dma("act", sbv[sq, 0:120, :], vs.t[8:128, :], rd=[vs.b])
                            else:
                                S.dma("sp", sbk[sq, kt * 128 - 8:kt * 128 + 120, :], ks.t[:, :], rd=[ks.b])
                                S.dma("act", sbv[sq, kt * 128 - 8:kt * 128 + 120, :], vs.t[:, :], rd=[vs.b])
                            for g in range(2):
                                bank = nxt("tr", TR)
                                for i in range(4):
                                    j = 4 * g + i
                                    tp(bank.t[:, i * 128:(i + 1) * 128], ks.t[:, j * 128:(j + 1) * 128], [ks.b], [bank.b])
                                dst = ktc.t[:, 4 * g:4 * g + 4, :]
                                srcp = bank.t[:, :].rearrange("p (a b) -> p a b", a=4)
                                if g == 0:
                                    S.op("act", lambda: A.activation(out=dst, in_=srcp, func=AF.Copy),
                                         rd=[bank.b], wr=[ktc.b])
                                else:
                                    S.op("dve", lambda: V.tensor_copy(out=dst, in_=srcp), rd=[bank.b], wr=[ktc.b])
                            S.op("pool", lambda: G.tensor_copy(out=vbb.t[:, :], in_=vs.t[:, :]), rd=[vs.b], wr=[vbb.b])
                            attend(bpairs,
                                   lambda jj, h: ktc.t[64 * h:64 * h + 64, jj, :], lambda jj: [ktc.b],
                                   lambda jj: vbb.t[:, jj * 128:(jj + 1) * 128], lambda jj: [vbb.b],
                                   msb.t[:, kt, :], msb.b, False)
                        attend(bpairs,
                               lambda jj, h: KT.t[64 * h:64 * h + 64, 8 + jj, :], lambda jj: [KTb[8 + jj][0]],
                               lambda jj: Vb.t[:, 8 + jj, :], lambda jj: [Vbb[8 + jj][0]],
                               mnb.t[:, sq, :], mnb.b, False)
                        k = sq % 2
                        ka, va, kd, vd, kta = kat[k], vat[k], kdup[k], vdup[k], KTA[k]
                        S.dma("sp", ka.t[:, :], cak[sq, :, :], wr=[ka.b])
                        S.dma("act", va.t[:, :], cav[sq, :, :], wr=[va.b])
                        S.dma("sp", sak[sq, 0:120, :], ka.t[8:128, :], rd=[ka.b])
                        S.dma("act", sav[sq, 0:120, :], va.t[8:128, :], rd=[va.b])
                        kab = ka.t[:, :].rearrange("p (k e) -> p k e", e=64).unsqueeze(2).broadcast_to([128, 4, 2, 64])
                        vab = va.t[:, :].rearrange("p (k e) -> p k e", e=64).unsqueeze(2).broadcast_to([128, 4, 2, 64])
                        S.op("dve", lambda: V.tensor_copy(out=kd.t[:, :, :, :], in_=kab), rd=[ka.b], wr=[kd.b])
                        S.op("pool", lambda: G.tensor_copy(out=vd.t[:, :, :, :], in_=vab), rd=[va.b], wr=[vd.b])
                        bank = nxt("tr", TR)
                        for kvh in range(4):
                            tp(bank.t[:, kvh * 128:(kvh + 1) * 128],
                               kd.t[:, kvh, :, :].rearrange("p a e -> p (a e)"), [kd.b], [bank.b])
                        S.op("act", lambda: A.activation(out=kta.t[:, :, :],
                                                         in_=bank.t[:, :].rearrange("p (a b) -> p a b", a=4),
                                                         func=AF.Copy), rd=[bank.b], wr=[kta.b])
                        attend(apairs,
                               lambda jj, h: kta.t[64 * h:64 * h + 64, jj // 2, :], lambda jj: [kta.b],
                               lambda jj: vd.t[:, jj // 2, :, :].rearrange("p a e -> p (a e)"), lambda jj: [vd.b],
                               msa.t[:, :], msa.b, False)
                        attend(apairs,
                               lambda jj, h: KT.t[64 * h:64 * h + 64, jj, :], lambda jj: [KTb[jj][0]],
                               lambda jj: Vb.t[:, jj, :], lambda jj: [Vbb[jj][0]],
                               mna.t[:, sq, :], mna.b, True)
                        k = sq % 2
                        dd, rdd = d2s[k], rds[k]
                        acc3 = acc.t[:, :].rearrange("p (j x) -> p j x", x=32)
                        den4 = acc3[:, :, 16:32].rearrange("p j (h t) -> p j h t", t=8)
                        es4 = esink.t[:, :].rearrange("p (j h) -> p j h", h=2).unsqueeze(3).broadcast_to([128, 16, 2, 8])
                        dd4 = dd.t[:, :].rearrange("p (j h t) -> p j h t", h=2, t=8)
                        rd4 = rdd.t[:, :].rearrange("p (j h t) -> p j h t", h=2, t=8)
                        S.op("dve", lambda: V.tensor_tensor(out=dd4, in0=den4, in1=es4, op=ALU.add),
                             rd=[acc.b, esink.b], wr=[dd.b])
                        S.op("dve", lambda: V.reciprocal(out=rdd.t[:, :], in_=dd.t[:, :]), rd=[dd.b], wr=[rdd.b])
                        S.op("dve", lambda: V.tensor_tensor(out=mixT.t[0:64, :, sq * 8:(sq + 1) * 8],
                                                            in0=acc3[0:64, :, 0:8], in1=rd4[0:64, :, 0, :], op=ALU.mult),
                             rd=[acc.b, rdd.b], wr=[mixT.b])
                        S.op("dve", lambda: V.tensor_tensor(out=mixT.t[64:128, :, sq * 8:(sq + 1) * 8],
                                                            in0=acc3[64:128, :, 8:16], in1=rd4[64:128, :, 1, :],
                                                            op=ALU.mult), rd=[acc.b, rdd.b], wr=[mixT.b])
                S.barrier()

            if STOP_AT < 4:
                return
            ntt = T // 128
            with ExitStack() as pb:
                zs = sb(pb, "zs", [128, NCH, T])
                zsb = [Buf() for _ in range(NCH)]
                hTb = sb(pb, "hTb", [128, NCH, T], BF16)
                hTbb = [Buf() for _ in range(NCH)]
                uT = [sb(pb, f"uT{i}", [128, 16, T], BF16) for i in range(2)]
                uTb = [[Buf() for _ in range(16)] for _ in range(2)]
                slb = [sb(pb, f"slabB{i}", [128, NCH, 128], BF16) for i in range(4)]
                xcols = [sb(pb, f"xcols{i}", [128, ntt, 128]) for i in range(2)]
                sq_ = [sb(pb, f"sq{i}", [128, T]) for i in range(2)]
                rl = [sb(pb, f"rl{i}", [128, T]) for i in range(2)]
                mean = sb(pb, "mean", [128, T])
                msq = sb(pb, "msq", [128, T])
                rstd = sb(pb, "rstd", [128, T])
                yst = [sb(pb, f"yst{i}", [128, 512]) for i in range(2)]
                ST1, ST2 = SC[0], SC[1]
                sl = {"i": 0, "q": 0}

                def next_slab():
                    s_ = slb[sl["i"] % len(slb)]
                    sl["i"] += 1
                    return s_

                def stats_accum(d):
                    k = sl["q"] % 2
                    sl["q"] += 1
                    q_ = sq_[k]
                    S.op("act", lambda: A.activation(out=q_.t[:, :], in_=zs.t[:, d, :], func=AF.Square),
                         rd=[zsb[d]], wr=[q_.b])
                    mm(ST1.t[:, 0:T], onesf.t[:, :], zs.t[:, d, :], d == 0, d == NCH - 1, [onesf.b, zsb[d]], [ST1.b])
                    mm(ST2.t[:, 0:T], onesf.t[:, :], q_.t[:, :], d == 0, d == NCH - 1, [onesf.b, q_.b], [ST2.b])

                def stats_final():
                    S.op("act", lambda: A.activation(out=mean.t[:, :], in_=ST1.t[:, 0:T], func=AF.Copy, scale=1.0 / D),
                         rd=[ST1.b], wr=[mean.b])
                    S.op("dve", lambda: V.tensor_tensor(out=msq.t[:, :], in0=mean.t[:, :], in1=mean.t[:, :], op=ALU.mult),
                         rd=[mean.b], wr=[msq.b])
                    S.op("dve", lambda: V.scalar_tensor_tensor(out=msq.t[:, :], in0=ST2.t[:, 0:T], scalar=1.0 / D,
                                                               in1=msq.t[:, :], op0=ALU.mult, op1=ALU.subtract),
                         rd=[ST2.b, msq.b], wr=[msq.b])
                    S.op("dve", lambda: V.tensor_scalar(out=msq.t[:, :], in0=msq.t[:, :], scalar1=LN_EPS, scalar2=None,
                                                        op0=ALU.add), rd=[msq.b], wr=[msq.b])
                    S.op("act", lambda: A.activation(out=msq.t[:, :], in_=msq.t[:, :], func=AF.Sqrt),
                         rd=[msq.b], wr=[msq.b])
                    S.op("dve", lambda: V.reciprocal(out=rstd.t[:, :], in_=msq.t[:, :]), rd=[msq.b], wr=[rstd.b])

                def normalize(d):
                    S.op("dve", lambda: V.tensor_tensor(out=zs.t[:, d, :], in0=zs.t[:, d, :], in1=mean.t[:, :],
                                                        op=ALU.subtract), rd=[zsb[d], mean.b], wr=[zsb[d]])
                    S.op("dve", lambda: V.tensor_tensor(out=zs.t[:, d, :], in0=zs.t[:, d, :], in1=rstd.t[:, :],
                                                        op=ALU.mult), rd=[zsb[d], rstd.b], wr=[zsb[d]])

                for d in range(NCH):
                    slab = next_slab()
                    S.dma("pool", slab.t[:, :, :], w_out_r[:, :, d * 128:(d + 1) * 128], wr=[slab.b], ndesc=128)
                    xc = xcols[d % 2]
                    if sample:
                        S.dma("sp", xc.t[:, 0, :], xs[:, d * 128:(d + 1) * 128], wr=[xc.b])
                    else:
                        r0 = (4 * step + NWT - NQT) * 128
                        S.dma("sp", xc.t[:, :, :],
                              xh[r0:r0 + T, d * 128:(d + 1) * 128].rearrange("(t p) c -> p t c", p=128), wr=[xc.b])
                    bank = nxt("pj", PJ)
                    for c in range(NCH):
                        mm(bank.t[:, 0:T], slab.t[:, c, :], mixT.t[:, c, 0:T], c == 0, False, [slab.b, mixT.b], [bank.b])
                    for t in range(ntt):
                        mm(bank.t[:, t * 128:(t + 1) * 128], xc.t[:, t, :], alphai.t[:, :], False, t == ntt - 1,
                           [xc.b, alphai.b], [bank.b])
                    S.op("act", lambda: A.activation(out=zs.t[:, d, :], in_=bank.t[:, 0:T], func=AF.Copy),
                         rd=[bank.b], wr=[zsb[d]])
                    stats_accum(d)
                stats_final()
                for d in range(NCH):
                    normalize(d)
                    S.op("act", lambda: A.activation(out=hTb.t[:, d, :], in_=zs.t[:, d, :], func=AF.Identity,
                                                     scale=lnp.t[:, 0, d:d + 1], bias=lnp.t[:, 1, d:d + 1]),
                         rd=[zsb[d], lnp.b], wr=[hTbb[d]])
                    S.op("act", lambda: A.activation(out=zs.t[:, d, :], in_=zs.t[:, d, :], func=AF.Identity,
                                                     scale=lnpa.t[:, 0, d:d + 1], bias=lnpa.t[:, 1, d:d + 1]),
                         rd=[zsb[d], lnpa.b], wr=[zsb[d]])
                for q4 in range(4):
                    u = uT[q4 % 2]
                    ub = uTb[q4 % 2]
                    for i in range(16):
                        f = 16 * q4 + i
                        slab = next_slab()
                        S.dma("pool", slab.t[:, :, :], w_up_r[:, :, f * 128:(f + 1) * 128], wr=[slab.b], ndesc=128)
                        bank = nxt("pj", PJ)
                        for c in range(NCH):
                            mm(bank.t[:, 0:T], slab.t[:, c, :], hTb.t[:, c, :], c == 0, c == NCH - 1,
                               [slab.b, hTbb[c]], [bank.b])
                        r_ = rl[i % 2]
                        S.op("act", lambda: A.activation(out=r_.t[:, :], in_=bank.t[:, 0:T], func=AF.Relu),
                             rd=[bank.b], wr=[r_.b])
                        S.op("pool", lambda: G.tensor_tensor(out=u.t[:, i, :], in0=r_.t[:, :], in1=r_.t[:, :], op=ALU.mult),
                             rd=[r_.b], wr=[ub[i]])
                    for d in range(NCH):
                        slab = next_slab()
                        S.dma("pool", slab.t[:, :, :], w_down_r[:, 16 * q4:16 * q4 + 16, d * 128:(d + 1) * 128],
                              wr=[slab.b], ndesc=128)
                        bank = nxt("pj", PJ)
                        for i in range(16):
                            mm(bank.t[:, 0:T], slab.t[:, i, :], u.t[:, i, :], i == 0, i == 15, [slab.b, ub[i]], [bank.b])
                        S.op("dve", lambda: V.tensor_tensor(out=zs.t[:, d, :], in0=bank.t[:, 0:T], in1=zs.t[:, d, :],
                                                            op=ALU.add), rd=[bank.b, zsb[d]], wr=[zsb[d]])
                for d in range(NCH):
                    stats_accum(d)
                stats_final()
                for d in range(NCH):
                    normalize(d)
                    S.op("act", lambda: A.activation(out=zs.t[:, d, :], in_=zs.t[:, d, :], func=AF.Identity,
                                                     scale=lnp.t[:, 2, d:d + 1], bias=lnp.t[:, 3, d:d + 1]),
                         rd=[zsb[d], lnp.b], wr=[zsb[d]])
                oc = 0
                for t in range(ntt):
                    for g in range(4):
                        bank = nxt("tr", TR)
                        for i in range(4):
                            d = 4 * g + i
                            tp(bank.t[:, i * 128:(i + 1) * 128], zs.t[:, d, t * 128:(t + 1) * 128], [zsb[d]], [bank.b])
                        ystg = yst[oc % 2]
                        if oc % 2 == 0:
                            S.op("act", lambda: A.activation(out=ystg.t[:, :], in_=bank.t[:, :], func=AF.Copy),
                                 rd=[bank.b], wr=[ystg.b])
                        else:
                            S.op("dve", lambda: V.tensor_copy(out=ystg.t[:, :], in_=bank.t[:, :]), rd=[bank.b], wr=[ystg.b])
                        oc += 1
                        if sample:
                            S.dma("sp", ys[:, g * 512:(g + 1) * 512], ystg.t[:, :], rd=[ystg.b])
                        else:
                            r0 = (step * NQT + t) * 128
                            S.dma("sp", y[r0:r0 + 128, g * 512:(g + 1) * 512], ystg.t[:, :], rd=[ystg.b])
                S.barrier()

        steps = list(range(NSTEP + 1)) if DEBUG_PARTS is None else DEBUG_PARTS
        for st in (steps if STOP_AT > 0 else []):
            emit_step(st)
        S.finish()
    return nc


def _tables(core):
    j = core % 4
    cs = j * 2048
    inv = (1.0 / (np.float32(10000.0) ** (np.arange(32, dtype=np.float32) / np.float32(32)))).astype(np.float32)
    c2 = np.zeros((128, 33, 64), np.float32)
    s2 = np.zeros((128, 33, 64), np.float32)
    p = np.arange(128)
    for tw in range(33):
        if tw < 32:
            pos = (cs - 2048 + tw * 128 + p).astype(np.float32)
        else:
            pos = (PAST + (p % 8)).astype(np.float32)
        ang = (pos[:, None] * inv[None, :]).astype(np.float32)
        co = np.cos(ang).astype(np.float32)
        si = np.sin(ang).astype(np.float32)
        c2[:, tw, 0:32] = co
        c2[:, tw, 32:64] = co
        s2[:, tw, 0:32] = -si
        s2[:, tw, 32:64] = si
    valid = np.ones((128, 4 * NWT + 1), np.float32)
    for s in range(NSTEP):
        for wt in range(NWT):
            gpos = cs - 2048 + (4 * s + wt) * 128
            if gpos < 0:
                valid[:, s * NWT + wt] = 0.0
    return c2.reshape(128, -1), s2.reshape(128, -1), valid


def _masks():
    k = np.arange(128)[:, None]
    q = np.arange(128)[None, :]
    dl = q - k
    m4 = (dl % 4 == 0).astype(np.float32)
    m16 = (dl % 16 == 0).astype(np.float32)
    le0 = (dl <= 0).astype(np.float32)
    ge0 = (dl >= 0).astype(np.float32)
    mk = np.zeros((128, 8, 128), np.float32)
    mk[:, 0] = ge0
    mk[:, 1] = (dl < 0).astype(np.float32)
    mk[:, 2] = ge0 * (1.0 + m4 + m16)
    mk[:, 3] = le0 + m4 + m16
    mk[:, 4] = m4 + m16
    mk[:, 5] = le0 * m4 + m16
    mk[:, 6] = m16
    mk[:, 7] = le0 * m16
    t = np.arange(8)[None, :]
    i = np.arange(128)[:, None]
    msb = np.zeros((128, 16, 8), np.float32)
    for kt in range(16):
        dist = 2048 + t - (128 * kt + i)
        msb[:, kt] = ((dist <= 128).astype(np.float32) + ((dist % 4 == 0) & (dist <= 512)).astype(np.float32)
                      + ((dist % 16 == 0) & (dist <= 2048)).astype(np.float32))
    mnb = np.zeros((128, 16, 8), np.float32)
    mna = np.zeros((128, 16, 8), np.float32)
    for sq in range(16):
        for tk in range(8):
            for tq in range(8):
                if tk <= tq:
                    d = tq - tk
                    mnb[sq * 8 + tk, sq, tq] = 1.0 + (d % 4 == 0) + (d % 16 == 0)
                    mna[sq * 8 + tk, sq, tq] = 1.0
    dist = 128 + t - i
    msa = ((dist >= 0) & (dist < 128)).astype(np.float32)
    return mk.reshape(128, -1), msb.reshape(128, -1), mnb.reshape(128, -1), msa, mna.reshape(128, -1)


_NC_CACHE = {}


def kernel(x_prompt, x_sample, cache_a_k, cache_a_v, cache_b_k, cache_b_v,
           w_in, sinks, w_out, ln1_g, ln1_b, w_up, w_down, ln2_g, ln2_b):
    f = lambda a: np.ascontiguousarray(np.asarray(a, dtype=np.float32))
    x_prompt, x_sample = f(x_prompt), f(x_sample)
    cache_a_k, cache_a_v, cache_b_k, cache_b_v = f(cache_a_k), f(cache_a_v), f(cache_b_k), f(cache_b_v)
    w_in_, w_out_, w_up_, w_down_ = f(w_in)[0], f(w_out)[0], f(w_up)[0], f(w_down)[0]
    sinks_ = f(sinks).reshape(1, 16)
    lnp = np.stack([f(ln1_g)[0], f(ln1_b)[0], f(ln2_g)[0], f(ln2_b)[0]], 0)
    lnp = np.ascontiguousarray(lnp.reshape(4, 16, 128).transpose(2, 0, 1)).reshape(128, 64)
    mk, msb, mnb, msa, mna = _masks()
    identf = np.eye(128, dtype=np.float32)
    alphai = (np.eye(128) * ALPHA).astype(np.float32)

    if "nc" not in _NC_CACHE:
        _NC_CACHE["nc"] = build_nc()
    nc = _NC_CACHE["nc"]

    full = DEBUG_PARTS is None or NSTEP in DEBUG_PARTS
    nsq_d = NSEQ if full else 1
    in_maps = []
    for c in range(8):
        b, j = c // 4, c % 4
        cs = j * 2048
        xh = np.zeros((4096, D), np.float32)
        lo = cs - 2048
        if lo < 0:
            xh[2048:] = x_prompt[b, 0:2048]
        else:
            xh[:] = x_prompt[b, lo:lo + 4096]
        sl = slice(c * NSEQ, (c + 1) * NSEQ)
        c2, s2, valid = _tables(c)
        in_maps.append({
            "xh": xh,
            "xs": np.ascontiguousarray(x_sample[sl].reshape(128, D)),
            "cak": np.ascontiguousarray(cache_a_k[0, sl].reshape(NSEQ, 128, 256)),
            "cav": np.ascontiguousarray(cache_a_v[0, sl].reshape(NSEQ, 128, 256)),
            "cbk": np.ascontiguousarray(cache_b_k[0, sl].reshape(NSEQ, 2048, 1024))[:nsq_d],
            "cbv": np.ascontiguousarray(cache_b_v[0, sl].reshape(NSEQ, 2048, 1024))[:nsq_d],
            "w_in": w_in_, "w_out": w_out_, "w_up": w_up_, "w_down": w_down_,
            "sinks": sinks_, "lnp": lnp, "c2": c2, "s2": s2, "valid": valid,
            "mk": mk, "msb": msb, "mnb": mnb, "msa": msa, "mna": mna,
            "identf": identf, "alphai": alphai,
        })
    res = run_bass_kernel_spmd(nc, in_maps, core_ids=list(range(8)))
    R = res.results
    y_prompt = np.zeros((2, 8192, D), np.float32)
    y_sample = np.zeros((128, 8, D), np.float32)
    pak = np.zeros((1, 2, 128, 4, 64), np.float32)
    pav = np.zeros_like(pak)
    pbk = np.zeros((1, 2, 2048, 16, 64), np.float32)
    pbv = np.zeros_like(pbk)
    sak = np.zeros((1, 128, 128, 4, 64), np.float32)
    sav = np.zeros_like(sak)
    sbk = np.zeros((1, 128, 2048, 16, 64), np.float32)
    sbv = np.zeros_like(sbk)
    for c in range(8):
        b, j = c // 4, c % 4
        r = R[c]
        y_prompt[b, j * 2048:(j + 1) * 2048] = r["y"]
        sl = slice(c * NSEQ, (c + 1) * NSEQ)
        y_sample[sl] = r["ys"].reshape(NSEQ, 8, D)
        if j == 3:
            pak[0, b] = r["pak"].reshape(128, 4, 64)
            pav[0, b] = r["pav"].reshape(128, 4, 64)
            pbk[0, b] = r["pbk"].reshape(2048, 16, 64)
            pbv[0, b] = r["pbv"].reshape(2048, 16, 64)
        sak[0, sl] = r["sak"].reshape(NSEQ, 128, 4, 64)
        sav[0, sl] = r["sav"].reshape(NSEQ, 128, 4, 64)
        if full:
            sbk[0, sl] = r["sbk"].reshape(NSEQ, 2048, 16, 64)
            sbv[0, sl] = r["sbv"].reshape(NSEQ, 2048, 16, 64)
    return (y_prompt, y_sample, pak, pav, pbk, pbv, sak, sav, sbk, sbv)
```

```python
import numpy as np
from contextlib import ExitStack
import concourse.bass as bass
import concourse.mybir as mybir
from concourse.bass_utils import run_bass_kernel_spmd

F32 = mybir.dt.float32
BF16 = mybir.dt.bfloat16
AF = mybir.ActivationFunctionType
ALU = mybir.AluOpType

D = 2048
NCH = 16
DFF = 8192
NQT = 4
NWT = 20
NSTEP = 4
PAST = 16384
SCALE = 0.125
LN_EPS = 1e-5
ALPHA = 2.0 ** 0.25
NSEQ = 16
DEBUG_PARTS = None
STOP_AT = 99
DBG_PAIRS = None
DBG_CUT = 0


class Buf:
    __slots__ = ("w", "r", "dsem", "dcnt", "name", "excl")

    def __init__(self, name=""):
        self.excl = False
        self.w = None
        self.r = {}
        self.dsem = None
        self.dcnt = 0
        self.name = name


class TB:
    __slots__ = ("t", "b")

    def __init__(self, t, name=""):
        self.t = t
        self.b = Buf(name)


class Sch:
    def __init__(self, nc, es):
        self.nc = nc
        self.es = es
        self.eng = {"pe": nc.tensor, "act": nc.scalar, "dve": nc.vector, "pool": nc.gpsimd, "sp": nc.sync}
        self.sem = {}
        self.cnt = {}
        self.seen = {}
        self.nsem = 0
        self.dbufs = []
        self.dpool = []
        self.dpool_sw = []
        self.dkind = {}
        self.pe_mode = None
        self.swq = []
        self.allsems = []
        for k in self.eng:
            self.sem[k] = self.newsem("p_" + k)
            self.cnt[k] = 0
            self.seen[k] = {}

    def newsem(self, name):
        self.nsem += 1
        s = self.es.enter_context(self.nc.semaphore(f"{name}_{self.nsem}"))
        return s

    def _wait(self, e, toks):
        best = {}
        for (sem, val) in toks:
            k = id(sem)
            if k not in best or best[k][1] < val:
                best[k] = (sem, val)
        for k, (sem, val) in best.items():
            if e == "pe" and sem is self.sem["pe"]:
                continue
            if self.seen[e].get(k, 0) >= val:
                continue
            self.eng[e].wait_ge(sem, val)
            self.seen[e][k] = val

    def _deps(self, rd, wr, dma_buf=None):
        toks = []
        for b in rd:
            if b.w is not None:
                toks.append(b.w)
        for b in wr:
            if b.w is not None:
                if not (dma_buf is not None and b is dma_buf and b.w[0] is b.dsem):
                    toks.append(b.w)
            toks.extend(b.r.values())
        return toks

    def _mark(self, tok, rd, wr):
        for b in wr:
            b.w = tok
            b.r = {}
        for b in rd:
            k = id(tok[0])
            if k not in b.r or b.r[k][1] < tok[1]:
                b.r[k] = tok

    def op(self, e, fn, rd=(), wr=()):
        if self.cnt[e] >= 30000:
            self.sem[e] = self.newsem("p_" + e)
            self.cnt[e] = 0
        ex = [b for b in rd if b.excl]
        if ex:
            wr = list(wr) + ex
        self._wait(e, self._deps(rd, wr))
        inst = fn()
        self.cnt[e] += 1
        inst.then_inc(self.sem[e], 1)
        self._mark((self.sem[e], self.cnt[e]), rd, wr)

    def dma(self, q, out, in_, rd=(), wr=(), ndesc=64, **kw):
        pb = wr[0] if len(wr) else rd[0]
        if q == "pool":
            while self.swq and sum(n for _, n in self.swq) + ndesc > 400:
                sem0 = self.swq[0][0][0]
                same = [t for t, _ in self.swq if t[0] is sem0]
                self.swq = [(t, n) for t, n in self.swq if t[0] is not sem0]
                self._wait("pool", [(sem0, max(v for _, v in same))])
        kind = "sw" if q == "pool" else "hw"
        if pb.dsem is None:
            pool = self.dpool_sw if kind == "sw" else self.dpool
            if pool:
                pb.dsem, pb.dcnt = pool.pop()
            else:
                pb.dsem = self.newsem("d" + kind)
            self.dbufs.append(pb)
            self.dkind[id(pb)] = kind
        assert self.dkind[id(pb)] == kind, "mixed SW/HW DGE on one semaphore"
        self._wait(q, self._deps(rd, wr, dma_buf=pb))
        pb.dcnt += 16
        self.eng[q].dma_start(out=out, in_=in_, **kw).then_inc(pb.dsem, 16)
        self._mark((pb.dsem, pb.dcnt), rd, wr)
        if q == "pool":
            self.swq.append(((pb.dsem, pb.dcnt), ndesc))

    def barrier(self):
        toks = [(self.sem[k], self.cnt[k]) for k in self.eng if self.cnt[k] > 0]
        toks += [(b.dsem, b.dcnt) for b in self.dbufs]
        for e in self.eng:
            self._wait(e, [t for t in toks if t[0] is not self.sem[e]])
        for b in self.dbufs:
            (self.dpool_sw if self.dkind[id(b)] == "sw" else self.dpool).append((b.dsem, b.dcnt))
            b.dsem = None
        self.dbufs = []
        self.dkind = {}
        self.swq = []

    def finish(self):
        toks = [(b.dsem, b.dcnt) for b in self.dbufs] + list(self.dpool) + list(self.dpool_sw)
        toks += [(self.sem[k], self.cnt[k]) for k in self.eng if self.cnt[k] > 0 and k != "sp"]
        self._wait("sp", toks)


def build_nc():
    nc = bass.Bass("TRN2", target_bir_lowering=False)

    def din(name, shape):
        return nc.dram_tensor(name, shape, F32, kind="ExternalInput").ap()

    def dout(name, shape):
        return nc.dram_tensor(name, shape, F32, kind="ExternalOutput").ap()

    xh = din("xh", [4096, D])
    xs = din("xs", [128, D])
    cak = din("cak", [NSEQ, 128, 256])
    cav = din("cav", [NSEQ, 128, 256])
    full = DEBUG_PARTS is None or NSTEP in DEBUG_PARTS
    nsq_d = NSEQ if full else 1
    cbk = din("cbk", [nsq_d, 2048, 1024])
    cbv = din("cbv", [nsq_d, 2048, 1024])
    w_in = din("w_in", [D, 4608])
    w_out = din("w_out", [D, D])
    w_up = din("w_up", [D, DFF])
    w_down = din("w_down", [DFF, D])
    sinks = din("sinks", [1, 16])
    lnp_d = din("lnp", [128, 4 * 16])
    c2_d = din("c2", [128, 33 * 64])
    s2_d = din("s2", [128, 33 * 64])
    valid_d = din("valid", [128, 4 * NWT + 1])
    mk_d = din("mk", [128, 8 * 128])
    msb_d = din("msb", [128, 16 * 8])
    mnb_d = din("mnb", [128, 16 * 8])
    msa_d = din("msa", [128, 8])
    mna_d = din("mna", [128, 16 * 8])
    identf_d = din("identf", [128, 128])
    alphai_d = din("alphai", [128, 128])

    y = dout("y", [2048, D])
    ys = dout("ys", [128, D])
    pak = dout("pak", [128, 256])
    pav = dout("pav", [128, 256])
    pbk = dout("pbk", [2048, 1024])
    pbv = dout("pbv", [2048, 1024])
    sak = dout("sak", [NSEQ, 128, 256])
    sav = dout("sav", [NSEQ, 128, 256])
    sbk = dout("sbk", [nsq_d, 2048, 1024])
    sbv = dout("sbv", [nsq_d, 2048, 1024])

    w_in_r = w_in.rearrange("(c p) n -> p c n", p=128)
    w_out_r = w_out.rearrange("(c p) n -> p c n", p=128)
    w_up_r = w_up.rearrange("(c p) n -> p c n", p=128)
    w_down_r = w_down.rearrange("(c p) n -> p c n", p=128)

    with ExitStack() as es:
        S = Sch(nc, es)
        V = nc.vector
        A = nc.scalar
        G = nc.gpsimd
        PE = nc.tensor

        uid = {"n": 0}

        def sb(stack, name, shape, dt=F32):
            uid["n"] += 1
            return TB(stack.enter_context(nc.sbuf_tensor(f"s{uid['n']}_{name}", shape, dt)), name)

        identf = sb(es, "identf", [128, 128])
        alphai = sb(es, "alphai", [128, 128])
        onesf = sb(es, "onesf", [128, 128])
        onesb = sb(es, "onesb", [128, 128], BF16)
        mk = sb(es, "mk", [128, 8, 128], BF16)
        msb = sb(es, "msb", [128, 16, 8], BF16)
        mnb = sb(es, "mnb", [128, 16, 8], BF16)
        msa = sb(es, "msa", [128, 8], BF16)
        mna = sb(es, "mna", [128, 16, 8], BF16)
        esink = sb(es, "esink", [128, 32])
        lnp = sb(es, "lnp", [128, 4, 16])
        lnpa = sb(es, "lnpa", [128, 2, 16])
        c2 = sb(es, "c2", [128, 33, 64])
        s2 = sb(es, "s2", [128, 33, 64])
        valid = sb(es, "valid", [128, 4 * NWT + 1])
        mixT = sb(es, "mixT", [128, 16, 512], BF16)
        banks = [TB(es.enter_context(nc.psum_tensor(f"bank{i}", [128, 512], F32)), f"bank{i}") for i in range(8)]
        for bk in banks:
            bk.b.excl = True
        PJ, TR, SC, AC = banks[0:2], banks[2:4], banks[4:6], banks[6:8]
        SCSETS = [(SC[0], SC[1]), (PJ[0], PJ[1]), (TR[0], TR[1])]

        S.dma("sp", identf.t[:, :], identf_d[:, :], wr=[identf.b])
        S.dma("sp", alphai.t[:, :], alphai_d[:, :], wr=[alphai.b])
        S.dma("sp", lnp.t[:, :, :], lnp_d.rearrange("p (k d) -> p k d", k=4), wr=[lnp.b])
        S.dma("sp", c2.t[:, :, :], c2_d.rearrange("p (w e) -> p w e", e=64), wr=[c2.b])
        S.dma("sp", s2.t[:, :, :], s2_d.rearrange("p (w e) -> p w e", e=64), wr=[s2.b])
        S.dma("sp", valid.t[:, :], valid_d[:, :], wr=[valid.b])
        S.dma("sp", esink.t[:, 0:16], sinks.partition_broadcast(128), wr=[esink.b])
        S.dma("pool", mk.t[:, :, :], mk_d.rearrange("p (m q) -> p m q", m=8), wr=[mk.b])
        S.dma("pool", msb.t[:, :, :], msb_d.rearrange("p (k t) -> p k t", t=8), wr=[msb.b])
        S.dma("pool", mnb.t[:, :, :], mnb_d.rearrange("p (k t) -> p k t", t=8), wr=[mnb.b])
        S.dma("pool", msa.t[:, :], msa_d[:, :], wr=[msa.b])
        S.dma("pool", mna.t[:, :, :], mna_d.rearrange("p (k t) -> p k t", t=8), wr=[mna.b])
        S.op("dve", lambda: V.memset(onesf.t[:, :], 1.0), wr=[onesf.b])
        S.op("dve", lambda: V.memset(onesb.t[:, :], 1.0), wr=[onesb.b])
        S.op("act", lambda: A.activation(out=esink.t[:, 0:16], in_=esink.t[:, 0:16], func=AF.Exp),
             rd=[esink.b], wr=[esink.b])
        S.op("dve", lambda: V.memset(esink.t[:, 16:32], 0.0), rd=[esink.b], wr=[esink.b])
        S.op("dve", lambda: V.tensor_scalar(out=lnpa.t[:, 0, :], in0=lnp.t[:, 0, :], scalar1=ALPHA, scalar2=None,
                                            op0=ALU.mult), rd=[lnp.b], wr=[lnpa.b])
        S.op("dve", lambda: V.tensor_scalar(out=lnpa.t[:, 1, :], in0=lnp.t[:, 1, :], scalar1=ALPHA, scalar2=None,
                                            op0=ALU.mult), rd=[lnp.b], wr=[lnpa.b])

        rr = {"pj": 0, "tr": 0, "sc": 0, "ac": 0}

        def nxt(key, lst):
            i = rr[key]
            rr[key] = i + 1
            return lst[i % len(lst)]

        def pe_mode(m):
            if S.pe_mode is not None and S.pe_mode != m:
                PE.drain()
            S.pe_mode = m

        def mm(out, lhsT, rhs, start, stop, rd, wr):
            pe_mode("mm")
            S.op("pe", lambda: PE.matmul(out, lhsT=lhsT, rhs=rhs, start=start, stop=stop, skip_group_check=True),
                 rd=rd, wr=wr)

        def tp(out, in_, rd, wr):
            pe_mode("tp")
            S.op("pe", lambda: PE.transpose(out, in_, identf.t[:, :]), rd=list(rd) + [identf.b], wr=wr)

        def emit_step(step):
            sample = step == NSTEP
            nwt = 1 if sample else NWT
            nqt = 1 if sample else NQT
            T = nqt * 128
            tb0 = 32 if sample else 4 * step
            vb0 = 4 * NWT if sample else NWT * step

            with ExitStack() as pa:
                xT = sb(pa, "xT", [128, NCH, nwt * 128], BF16)
                xTb = [Buf() for _ in range(nwt)]
                xst = [sb(pa, f"xst{i}", [128, D]) for i in range(2)]
                slabs = [sb(pa, f"slabA{i}", [128, NCH, 384], BF16) for i in range(2)]
                pst = [sb(pa, f"pst{i}", [128, 384]) for i in range(2)]
                t1 = [sb(pa, f"t1_{i}", [128, 256]) for i in range(2)]
                t2 = [sb(pa, f"t2_{i}", [128, 256]) for i in range(2)]
                kr = [sb(pa, f"kr{i}", [128, 256]) for i in range(2)]
                Pb = [sb(pa, f"P{i}", [128, 2, 512], BF16) for i in range(3)]
                d2 = [sb(pa, f"d2_{i}", [128, 256]) for i in range(2)]
                rdn = [sb(pa, f"rdn{i}", [128, 256]) for i in range(2)]
                if sample:
                    npr = 16
                    KT = sb(pa, "KT", [128, npr, 128], BF16)
                    QT = sb(pa, "QT", [128, npr, 128], BF16)
                    Vb = sb(pa, "Vb", [128, npr, 128], BF16)
                    KTb = [[Buf()] for _ in range(npr)]
                    QTb = [[Buf()] for _ in range(npr)]
                    Vbb = [[Buf()] for _ in range(npr)]
                else:
                    KT = sb(pa, "KT", [128, nwt * 128], BF16)
                    QT = sb(pa, "QT", [128, nqt * 128], BF16)
                    Vb = sb(pa, "Vb", [128, nwt, 128], BF16)
                    KTb = [Buf() for _ in range(nwt)]
                    QTb = [Buf() for _ in range(nqt)]
                    Vbb = [Buf() for _ in range(nwt)]

                for wt in range(nwt):
                    st = xst[wt % 2]
                    src = xs[:, :] if sample else xh[(4 * step + wt) * 128:(4 * step + wt + 1) * 128, :]
                    S.dma("sp", st.t[:, :], src, wr=[st.b])
                    for g in range(4):
                        bank = nxt("tr", TR)
                        for i in range(4):
                            c = 4 * g + i
                            tp(bank.t[:, i * 128:(i + 1) * 128], st.t[:, c * 128:(c + 1) * 128], [st.b], [bank.b])
                        dst = xT.t[:, 4 * g:4 * g + 4, wt * 128:(wt + 1) * 128]
                        srcp = bank.t[:, :].rearrange("p (a b) -> p a b", a=4)
                        if g % 2 == 0:
                            S.op("act", lambda: A.activation(out=dst, in_=srcp, func=AF.Copy), rd=[bank.b], wr=[xTb[wt]])
                        else:
                            S.op("dve", lambda: V.tensor_copy(out=dst, in_=srcp), rd=[bank.b], wr=[xTb[wt]])

                cnt = {"i": 0, "p": 0}
                for pr in (DBG_PAIRS if DBG_PAIRS is not None else range(16 if STOP_AT > 1 else 0)):
                    isA = pr < 8
                    slab = slabs[pr % 2]
                    if isA:
                        kv = pr // 2
                        qc = pr * 128
                        S.dma("pool", slab.t[:, :, 0:128], w_in_r[:, :, qc:qc + 128], wr=[slab.b], ndesc=128)
                        S.dma("pool", slab.t[:, :, 128:192], w_in_r[:, :, 1024 + 64 * kv:1088 + 64 * kv], wr=[slab.b],
                              ndesc=128)
                        S.dma("pool", slab.t[:, :, 256:320], w_in_r[:, :, 1280 + 64 * kv:1344 + 64 * kv], wr=[slab.b],
                              ndesc=128)
                        S.op("dve", lambda: V.tensor_copy(out=slab.t[:, :, 192:256], in_=slab.t[:, :, 128:192]),
                             rd=[slab.b], wr=[slab.b])
                        S.op("dve", lambda: V.tensor_copy(out=slab.t[:, :, 320:384], in_=slab.t[:, :, 256:320]),
                             rd=[slab.b], wr=[slab.b])
                    else:
                        hp = pr - 8
                        for s3 in range(3):
                            cc = 1536 + 1024 * s3 + hp * 128
                            S.dma("pool", slab.t[:, :, 128 * s3:128 * s3 + 128], w_in_r[:, :, cc:cc + 128], wr=[slab.b],
                                  ndesc=128)
                    wts = range(nwt) if (not isA or sample) else range(nwt - nqt - 1, nwt)
                    for wt in wts:
                        isq = wt >= nwt - nqt
                        qi = wt - (nwt - nqt)
                        c0 = 0 if isq else 128
                        bank = nxt("pj", PJ)
                        for c in range(NCH):
                            mm(bank.t[:, c0:384], xT.t[:, c, wt * 128:(wt + 1) * 128], slab.t[:, c, c0:384],
                               c == 0, c == NCH - 1, [xTb[wt], slab.b], [bank.b])
                        k = cnt["i"] % 2
                        cnt["i"] += 1
                        ps, a1, a2, ak = pst[k], t1[k], t2[k], kr[k]
                        S.op("act", lambda: A.activation(out=ps.t[:, c0:384], in_=bank.t[:, c0:384], func=AF.Copy),
                             rd=[bank.b], wr=[ps.b])
                        if DBG_CUT == 1:
                            continue
                        nh = (256 - c0) // 64
                        src3 = ps.t[:, c0:256].rearrange("p (h e) -> p h e", e=64)
                        tw = tb0 + wt
                        c2b = c2.t[:, tw, :].unsqueeze(1).broadcast_to([128, nh, 64])
                        s2lo = s2.t[:, tw, 0:32].unsqueeze(1).broadcast_to([128, nh, 32])
                        s2hi = s2.t[:, tw, 32:64].unsqueeze(1).broadcast_to([128, nh, 32])
                        a13 = a1.t[:, c0:256].rearrange("p (h e) -> p h e", e=64)
                        a23 = a2.t[:, c0:256].rearrange("p (h e) -> p h e", e=64)
                        S.op("dve", lambda: V.tensor_tensor(out=a13, in0=src3, in1=c2b, op=ALU.mult),
                             rd=[ps.b, c2.b], wr=[a1.b])
                        S.op("dve", lambda: V.tensor_tensor(out=a23[:, :, 0:32], in0=src3[:, :, 32:64], in1=s2lo,
                                                            op=ALU.mult), rd=[ps.b, s2.b], wr=[a2.b])
                        S.op("dve", lambda: V.tensor_tensor(out=a23[:, :, 32:64], in0=src3[:, :, 0:32], in1=s2hi,
                                                            op=ALU.mult), rd=[ps.b, s2.b], wr=[a2.b])
                        S.op("dve", lambda: V.tensor_tensor(out=ak.t[:, c0:256], in0=a1.t[:, c0:256],
                                                            in1=a2.t[:, c0:256], op=ALU.add),
                             rd=[a1.b, a2.b], wr=[ak.b])
                        if DBG_CUT == 3:
                            continue
                        bank2 = nxt("tr", TR)
                        tp(bank2.t[:, 0:128], ak.t[:, 128:256], [ak.b], [bank2.b])
                        if isq:
                            tp(bank2.t[:, 128:256], ak.t[:, 0:128], [ak.b], [bank2.b])
                        if DBG_CUT == 4:
                            continue
                        if sample:
                            kdst, kb_ = KT.t[:, pr, :], KTb[pr][0]
                            qdst, qb_ = QT.t[:, pr, :], QTb[pr][0]
                            vdst, vb_ = Vb.t[:, pr, :], Vbb[pr][0]
                        else:
                            kdst, kb_ = KT.t[:, wt * 128:(wt + 1) * 128], KTb[wt]
                            vdst, vb_ = Vb.t[:, wt, :], Vbb[wt]
                            if isq:
                                qdst, qb_ = QT.t[:, qi * 128:(qi + 1) * 128], QTb[qi]
                        S.op("act", lambda: A.activation(out=kdst, in_=bank2.t[:, 0:128], func=AF.Copy),
                             rd=[bank2.b], wr=[kb_])
                        if DBG_CUT == 5:
                            continue
                        if isq:
                            S.op("dve", lambda: V.tensor_copy(out=qdst, in_=bank2.t[:, 128:256]),
                                 rd=[bank2.b], wr=[qb_])
                        if DBG_CUT == 6:
                            continue
                        S.op("act", lambda: A.activation(out=vdst, in_=ps.t[:, 256:384], func=AF.Copy), rd=[ps.b], wr=[vb_])
                        if isq and not sample and DBG_CUT != 2:
                            r0 = (step * NQT + qi) * 128
                            if not isA:
                                hp = pr - 8
                                S.dma("sp", pbk[r0:r0 + 128, hp * 128:(hp + 1) * 128], ak.t[:, 128:256], rd=[ak.b])
                                S.dma("sp", pbv[r0:r0 + 128, hp * 128:(hp + 1) * 128], ps.t[:, 256:384], rd=[ps.b])
                            elif pr % 2 == 0 and step == NSTEP - 1 and qi == NQT - 1:
                                kv = pr // 2
                                S.dma("sp", pak[:, kv * 64:(kv + 1) * 64], ak.t[:, 128:192], rd=[ak.b])
                                S.dma("sp", pav[:, kv * 64:(kv + 1) * 64], ps.t[:, 256:320], rd=[ps.b])
                        if sample:
                            for sq in range(NSEQ):
                                if not isA:
                                    hp = pr - 8
                                    S.dma("sp", sbk[sq, 2040:2048, hp * 128:(hp + 1) * 128],
                                          ak.t[sq * 8:(sq + 1) * 8, 128:256], rd=[ak.b])
                                    S.dma("sp", sbv[sq, 2040:2048, hp * 128:(hp + 1) * 128],
                                          ps.t[sq * 8:(sq + 1) * 8, 256:384], rd=[ps.b])
                                elif pr % 2 == 0:
                                    kv = pr // 2
                                    S.dma("sp", sak[sq, 120:128, kv * 64:(kv + 1) * 64],
                                          ak.t[sq * 8:(sq + 1) * 8, 128:192], rd=[ak.b])
                                    S.dma("sp", sav[sq, 120:128, kv * 64:(kv + 1) * 64],
                                          ps.t[sq * 8:(sq + 1) * 8, 256:320], rd=[ps.b])

                    if sample or STOP_AT < 3:
                        continue
                    for qi in range(nqt):
                        wq = nwt - nqt + qi
                        if isA:
                            kts = [(wq, 0), (wq - 1, 1)]
                        else:
                            def mt_of(dl):
                                return 2 if dl == 0 else 3 if dl == 1 else 4 if dl in (2, 3) else 5 if dl == 4 else \
                                    7 if dl == 16 else 6
                            kts = [(wq - dl, mt_of(dl)) for dl in range(17)]
                        acc = nxt("ac", AC)
                        first = True
                        for c0_ in range(0, len(kts), 4):
                            chunk = kts[c0_:c0_ + 4]
                            scs = nxt("sc", SCSETS)
                            for a_, (kt, mt) in enumerate(chunk):
                                for h in range(2):
                                    mm(scs[h].t[:, a_ * 128:(a_ + 1) * 128],
                                       KT.t[64 * h:64 * h + 64, kt * 128:(kt + 1) * 128],
                                       QT.t[64 * h:64 * h + 64, qi * 128:(qi + 1) * 128], True, True,
                                       [KTb[kt], QTb[qi]], [scs[h].b])
                            n = len(chunk) * 128
                            P = Pb[cnt["p"] % 3]
                            cnt["p"] += 1
                            for h in range(2):
                                S.op("act", lambda: A.activation(out=P.t[:, h, 0:n], in_=scs[h].t[:, 0:n], func=AF.Exp,
                                                                 scale=SCALE), rd=[scs[h].b], wr=[P.b])
                            for a_, (kt, mt) in enumerate(chunk):
                                pv = P.t[:, :, a_ * 128:(a_ + 1) * 128]
                                mkb = mk.t[:, mt, :].unsqueeze(1).broadcast_to([128, 2, 128])
                                S.op("dve", lambda: V.scalar_tensor_tensor(
                                    out=pv, in0=pv, scalar=valid.t[:, vb0 + kt:vb0 + kt + 1], in1=mkb,
                                    op0=ALU.mult, op1=ALU.mult), rd=[P.b, valid.b, mk.b], wr=[P.b])
                            for a_, (kt, mt) in enumerate(chunk):
                                last = (c0_ + a_ == len(kts) - 1)
                                mm(acc.t[:, 0:128], Vb.t[:, kt, :], P.t[:, 0, a_ * 128:(a_ + 1) * 128], first, False,
                                   [Vbb[kt], P.b], [acc.b])
                                first = False
                                mm(acc.t[:, 128:256], Vb.t[:, kt, :], P.t[:, 1, a_ * 128:(a_ + 1) * 128], False, False,
                                   [Vbb[kt], P.b], [acc.b])
                                mm(acc.t[:, 256:512], onesb.t[:, :], P.t[:, :, a_ * 128:(a_ + 1) * 128], False, last,
                                   [onesb.b, P.b], [acc.b])
                        k = cnt["i"] % 2
                        cnt["i"] += 1
                        dd, rdd = d2[k], rdn[k]
                        esb = esink.t[:, 2 * pr:2 * pr + 2].unsqueeze(2).broadcast_to([128, 2, 128])
                        S.op("dve", lambda: V.tensor_tensor(out=dd.t[:, :].rearrange("p (h q) -> p h q", h=2),
                                                            in0=acc.t[:, 256:512].rearrange("p (h q) -> p h q", h=2),
                                                            in1=esb, op=ALU.add), rd=[acc.b, esink.b], wr=[dd.b])
                        S.op("dve", lambda: V.reciprocal(out=rdd.t[:, :], in_=dd.t[:, :]), rd=[dd.b], wr=[rdd.b])
                        S.op("dve", lambda: V.tensor_tensor(out=mixT.t[0:64, pr, qi * 128:(qi + 1) * 128],
                                                            in0=acc.t[0:64, 0:128], in1=rdd.t[0:64, 0:128], op=ALU.mult),
                             rd=[acc.b, rdd.b], wr=[mixT.b])
                        S.op("dve", lambda: V.tensor_tensor(out=mixT.t[64:128, pr, qi * 128:(qi + 1) * 128],
                                                            in0=acc.t[64:128, 128:256], in1=rdd.t[64:128, 128:256],
                                                            op=ALU.mult), rd=[acc.b, rdd.b], wr=[mixT.b])

                if sample:
                    kst = [sb(pa, f"kst{i}", [128, 1024]) for i in range(2)]
                    vst = [sb(pa, f"vst{i}", [128, 1024]) for i in range(2)]
                    vbf = [sb(pa, f"vbf{i}", [128, 1024], BF16) for i in range(2)]
                    KTc = [sb(pa, f"KTc{i}", [128, 8, 128], BF16) for i in range(2)]
                    kat = [sb(pa, f"kat{i}", [128, 256]) for i in range(2)]
                    vat = [sb(pa, f"vat{i}", [128, 256]) for i in range(2)]
                    kdup = [sb(pa, f"kdup{i}", [128, 4, 2, 64]) for i in range(2)]
                    vdup = [sb(pa, f"vdup{i}", [128, 4, 2, 64], BF16) for i in range(2)]
                    KTA = [sb(pa, f"KTA{i}", [128, 4, 128], BF16) for i in range(2)]
                    d2s = [sb(pa, f"d2s{i}", [128, 256]) for i in range(2)]
                    rds = [sb(pa, f"rds{i}", [128, 256]) for i in range(2)]
                    tcnt = {"t": 0}
                    for sq in range(NSEQ):
                        acc = nxt("ac", AC)
                        state = {"first": True}

                        def attend(pairs, ktf, ktbufs, vf, vbufs, mask_ap, mask_b, last):
                            scs = nxt("sc", SCSETS)
                            npairs = len(pairs)
                            for jj, pr_ in enumerate(pairs):
                                for h in range(2):
                                    mm(scs[h].t[:, jj * 8:(jj + 1) * 8], ktf(jj, h),
                                       QT.t[64 * h:64 * h + 64, pr_, sq * 8:(sq + 1) * 8], True, True,
                                       list(ktbufs(jj)) + [QTb[pr_][0]], [scs[h].b])
                            n = npairs * 8
                            P = Pb[cnt["p"] % 3]
                            cnt["p"] += 1
                            for h in range(2):
                                S.op("act", lambda: A.activation(out=P.t[:, h, 0:n], in_=scs[h].t[:, 0:n], func=AF.Exp,
                                                                 scale=SCALE), rd=[scs[h].b], wr=[P.b])
                            pv = P.t[:, :, 0:n].rearrange("p h (j t) -> p h j t", t=8)
                            mb = mask_ap.unsqueeze(1).unsqueeze(1).broadcast_to([128, 2, npairs, 8])
                            S.op("dve", lambda: V.tensor_tensor(out=pv, in0=pv, in1=mb, op=ALU.mult),
                                 rd=[P.b, mask_b], wr=[P.b])
                            for jj, pr_ in enumerate(pairs):
                                lst = last and jj == npairs - 1
                                mm(acc.t[:, pr_ * 32:pr_ * 32 + 8], vf(jj), P.t[:, 0, 8 * jj:8 * jj + 8],
                                   state["first"], False, list(vbufs(jj)) + [P.b], [acc.b])
                                state["first"] = False
                                mm(acc.t[:, pr_ * 32 + 8:pr_ * 32 + 16], vf(jj), P.t[:, 1, 8 * jj:8 * jj + 8],
                                   False, False, list(vbufs(jj)) + [P.b], [acc.b])
                                mm(acc.t[:, pr_ * 32 + 16:pr_ * 32 + 32], onesb.t[:, :], P.t[:, :, 8 * jj:8 * jj + 8],
                                   False, lst, [onesb.b, P.b], [acc.b])

                        bpairs = list(range(8, 16))
                        apairs = list(range(0, 8))
                        for kt in range(16):
                            k = tcnt["t"] % 2
                            tcnt["t"] += 1
                            ks, vs, vbb, ktc = kst[k], vst[k], vbf[k], KTc[k]
                            S.dma("sp", ks.t[:, :], cbk[sq, kt * 128:(kt + 1) * 128, :], wr=[ks.b])
                            S.dma("act", vs.t[:, :], cbv[sq, kt * 128:(kt + 1) * 128, :], wr=[vs.b])
                            if kt == 0:
                                S.dma("sp", sbk[sq, 0:120, :], ks.t[8:128, :], rd=[ks.b])
                                S.dma("act", sbv[sq, 0:120, :], vs.t[8:128, :], rd=[vs.b])
                            else:
                                S.dma("sp", sbk[sq, kt * 128 - 8:kt * 128 + 120, :], ks.t[:, :], rd=[ks.b])
                                S.dma("act", sbv[sq, kt * 128 - 8:kt * 128 + 120, :], vs.t[:, :], rd=[vs.b])
                            for g in range(2):
                                bank = nxt("tr", TR)
                                for i in range(4):
                                    j = 4 * g + i
                                    tp(bank.t[:, i * 128:(i + 1) * 128], ks.t[:, j * 128:(j + 1) * 128], [ks.b], [bank.b])
                                dst = ktc.t[:, 4 * g:4 * g + 4, :]
                                srcp = bank.t[:, :].rearrange("p (a b) -> p a b", a=4)
                                if g == 0:
                                    S.op("act", lambda: A.activation(out=dst, in_=srcp, func=AF.Copy),
                                         rd=[bank.b], wr=[ktc.b])
                                else:
                                    S.op("dve", lambda: V.tensor_copy(out=dst, in_=srcp), rd=[bank.b], wr=[ktc.b])
                            S.op("pool", lambda: G.tensor_copy(out=vbb.t[:, :], in_=vs.t[:, :]), rd=[vs.b], wr=[vbb.b])
                            attend(bpairs,
                                   lambda jj, h: ktc.t[64 * h:64 * h + 64, jj, :], lambda jj: [ktc.b],
                                   lambda jj: vbb.t[:, jj * 128:(jj + 1) * 128], lambda jj: [vbb.b],
                                   msb.t[:, kt, :], msb.b, False)
                        attend(bpairs,
                               lambda jj, h: KT.t[64 * h:64 * h + 64, 8 + jj, :], lambda jj: [KTb[8 + jj][0]],
                               lambda jj: Vb.t[:, 8 + jj, :], lambda jj: [Vbb[8 + jj][0]],
                               mnb.t[:, sq, :], mnb.b, False)
                        k = sq % 2
                        ka, va, kd, vd, kta = kat[k], vat[k], kdup[k], vdup[k], KTA[k]
                        S.dma("sp", ka.t[:, :], cak[sq, :, :], wr=[ka.b])
                        S.dma("act", va.t[:, :], cav[sq, :, :], wr=[va.b])
                        S.dma("sp", sak[sq, 0:120, :], ka.t[8:128, :], rd=[ka.b])
                        S.dma("act", sav[sq, 0:120, :], va.t[8:128, :], rd=[va.b])
                        kab = ka.t[:, :].rearrange("p (k e) -> p k e", e=64).unsqueeze(2).broadcast_to([128, 4, 2, 64])
                        vab = va.t[:, :].rearrange("p (k e) -> p k e", e=64).unsqueeze(2).broadcast_to([128, 4, 2, 64])
                        S.op("dve", lambda: V.tensor_copy(out=kd.t[:, :, :, :], in_=kab), rd=[ka.b], wr=[kd.b])
                        S.op("pool", lambda: G.tensor_copy(out=vd.t[:, :, :, :], in_=vab), rd=[va.b], wr=[vd.b])
                        bank = nxt("tr", TR)
                        for kvh in range(4):
                            tp(bank.t[:, kvh * 128:(kvh + 1) * 128],
                               kd.t[:, kvh, :, :].rearrange("p a e -> p (a e)"), [kd.b], [bank.b])
                        S.op("act", lambda: A.activation(out=kta.t[:, :, :],
                                                         in_=bank.t[:, :].rearrange("p (a b) -> p a b", a=4),
                                                         func=AF.Copy), rd=[bank.b], wr=[kta.b])
                        attend(apairs,
                               lambda jj, h: kta.t[64 * h:64 * h + 64, jj // 2, :], lambda jj: [kta.b],
                               lambda jj: vd.t[:, jj // 2, :, :].rearrange("p a e -> p (a e)"), lambda jj: [vd.b],
                               msa.t[:, :], msa.b, False)
                        attend(apairs,
                               lambda jj, h: KT.t[64 * h:64 * h + 64, jj, :], lambda jj: [KTb[jj][0]],
                               lambda jj: Vb.t[:, jj, :], lambda jj: [Vbb[jj][0]],
                               mna.t[:, sq, :], mna.b, True)
                        k = sq % 2
                        dd, rdd = d2s[k], rds[k]
                        acc3 = acc.t[:, :].rearrange("p (j x) -> p j x", x=32)
                        den4 = acc3[:, :, 16:32].rearrange("p j (h t) -> p j h t", t=8)
                        es4 = esink.t[:, :].rearrange("p (j h) -> p j h", h=2).unsqueeze(3).broadcast_to([128, 16, 2, 8])
                        dd4 = dd.t[:, :].rearrange("p (j h t) -> p j h t", h=2, t=8)
                        rd4 = rdd.t[:, :].rearrange("p (j h t) -> p j h t", h=2, t=8)
                        S.op("dve", lambda: V.tensor_tensor(out=dd4, in0=den4, in1=es4, op=ALU.add),
                             rd=[acc.b, esink.b], wr=[dd.b])
                        S.op("dve", lambda: V.reciprocal(out=rdd.t[:, :], in_=dd.t[:, :]), rd=[dd.b], wr=[rdd.b])
                        S.op("dve", lambda: V.tensor_tensor(out=mixT.t[0:64, :, sq * 8:(sq + 1) * 8],
                                                            in0=acc3[0:64, :, 0:8], in1=rd4[0:64, :, 0, :], op=ALU.mult),
                             rd=[acc.b, rdd.b], wr=[mixT.b])
                        S.op("dve", lambda: V.tensor_tensor(out=mixT.t[64:128, :, sq * 8:(sq + 1) * 8],
                                                            in0=acc3[64:128, :, 8:16], in1=rd4[64:128, :, 1, :],
                                                            op=ALU.mult), rd=[acc.b, rdd.b], wr=[mixT.b])
                S.barrier()

            if STOP_AT < 4:
                return
            ntt = T // 128
            with ExitStack() as pb:
                zs = sb(pb, "zs", [128, NCH, T])
                zsb = [Buf() for _ in range(NCH)]
                hTb = sb(pb, "hTb", [128, NCH, T], BF16)
                hTbb = [Buf() for _ in range(NCH)]
                uT = [sb(pb, f"uT{i}", [128, 16, T], BF16) for i in range(2)]
                uTb = [[Buf() for _ in range(16)] for _ in range(2)]
                slb = [sb(pb, f"slabB{i}", [128, NCH, 256], BF16) for i in range(4)]
                xcols = [sb(pb, f"xcols{i}", [128, ntt, 128]) for i in range(2)]
                sq_ = [sb(pb, f"sq{i}", [128, T]) for i in range(2)]
                rl = [sb(pb, f"rl{i}", [128, T]) for i in range(2)]
                mean = sb(pb, "mean", [128, T])
                msq = sb(pb, "msq", [128, T])
                rstd = sb(pb, "rstd", [128, T])
                yst = [sb(pb, f"yst{i}", [128, 512]) for i in range(2)]
                ST1, ST2 = SC[0], SC[1]
                sl = {"i": 0, "q": 0}

                def next_slab():
                    s_ = slb[sl["i"] % len(slb)]
                    sl["i"] += 1
                    return s_

                def stats_accum(d):
                    k = sl["q"] % 2
                    sl["q"] += 1
                    q_ = sq_[k]
                    S.op("act", lambda: A.activation(out=q_.t[:, :], in_=zs.t[:, d, :], func=AF.Square),
                         rd=[zsb[d]], wr=[q_.b])
                    mm(ST1.t[:, 0:T], onesf.t[:, :], zs.t[:, d, :], d == 0, d == NCH - 1, [onesf.b, zsb[d]], [ST1.b])
                    mm(ST2.t[:, 0:T], onesf.t[:, :], q_.t[:, :], d == 0, d == NCH - 1, [onesf.b, q_.b], [ST2.b])

                def stats_final():
                    S.op("act", lambda: A.activation(out=mean.t[:, :], in_=ST1.t[:, 0:T], func=AF.Copy, scale=1.0 / D),
                         rd=[ST1.b], wr=[mean.b])
                    S.op("dve", lambda: V.tensor_tensor(out=msq.t[:, :], in0=mean.t[:, :], in1=mean.t[:, :], op=ALU.mult),
                         rd=[mean.b], wr=[msq.b])
                    S.op("dve", lambda: V.scalar_tensor_tensor(out=msq.t[:, :], in0=ST2.t[:, 0:T], scalar=1.0 / D,
                                                               in1=msq.t[:, :], op0=ALU.mult, op1=ALU.subtract),
                         rd=[ST2.b, msq.b], wr=[msq.b])
                    S.op("dve", lambda: V.tensor_scalar(out=msq.t[:, :], in0=msq.t[:, :], scalar1=LN_EPS, scalar2=None,
                                                        op0=ALU.add), rd=[msq.b], wr=[msq.b])
                    S.op("act", lambda: A.activation(out=msq.t[:, :], in_=msq.t[:, :], func=AF.Sqrt),
                         rd=[msq.b], wr=[msq.b])
                    S.op("dve", lambda: V.reciprocal(out=rstd.t[:, :], in_=msq.t[:, :]), rd=[msq.b], wr=[rstd.b])

                def normalize(d):
                    S.op("dve", lambda: V.tensor_tensor(out=zs.t[:, d, :], in0=zs.t[:, d, :], in1=mean.t[:, :],
                                                        op=ALU.subtract), rd=[zsb[d], mean.b], wr=[zsb[d]])
                    S.op("dve", lambda: V.tensor_tensor(out=zs.t[:, d, :], in0=zs.t[:, d, :], in1=rstd.t[:, :],
                                                        op=ALU.mult), rd=[zsb[d], rstd.b], wr=[zsb[d]])

                for d in range(NCH):
                    if d % 2 == 0:
                        slab = next_slab()
                        S.dma("pool", slab.t[:, :, :], w_out_r[:, :, d * 128:(d + 2) * 128], wr=[slab.b], ndesc=128)
                    hf = (d % 2) * 128
                    xc = xcols[d % 2]
                    if sample:
                        S.dma("sp", xc.t[:, 0, :], xs[:, d * 128:(d + 1) * 128], wr=[xc.b])
                    else:
                        r0 = (4 * step + NWT - NQT) * 128
                        S.dma("sp", xc.t[:, :, :],
                              xh[r0:r0 + T, d * 128:(d + 1) * 128].rearrange("(t p) c -> p t c", p=128), wr=[xc.b])
                    bank = nxt("pj", PJ)
                    for c in range(NCH):
                        mm(bank.t[:, 0:T], slab.t[:, c, hf:hf + 128], mixT.t[:, c, 0:T], c == 0, False, [slab.b, mixT.b], [bank.b])
                    for t in range(ntt):
                        mm(bank.t[:, t * 128:(t + 1) * 128], xc.t[:, t, :], alphai.t[:, :], False, t == ntt - 1,
                           [xc.b, alphai.b], [bank.b])
                    S.op("act", lambda: A.activation(out=zs.t[:, d, :], in_=bank.t[:, 0:T], func=AF.Copy),
                         rd=[bank.b], wr=[zsb[d]])
                    stats_accum(d)
                stats_final()
                for d in range(NCH):
                    normalize(d)
                    S.op("act", lambda: A.activation(out=hTb.t[:, d, :], in_=zs.t[:, d, :], func=AF.Identity,
                                                     scale=lnp.t[:, 0, d:d + 1], bias=lnp.t[:, 1, d:d + 1]),
                         rd=[zsb[d], lnp.b], wr=[hTbb[d]])
                    S.op("act", lambda: A.activation(out=zs.t[:, d, :], in_=zs.t[:, d, :], func=AF.Identity,
                                                     scale=lnpa.t[:, 0, d:d + 1], bias=lnpa.t[:, 1, d:d + 1]),
                         rd=[zsb[d], lnpa.b], wr=[zsb[d]])
                for q4 in range(4):
                    u = uT[q4 % 2]
                    ub = uTb[q4 % 2]
                    for i in range(16):
                        f = 16 * q4 + i
                        if i % 2 == 0:
                            slab = next_slab()
                            S.dma("pool", slab.t[:, :, :], w_up_r[:, :, f * 128:(f + 2) * 128], wr=[slab.b], ndesc=128)
                        hf = (i % 2) * 128
                        bank = nxt("pj", PJ)
                        for c in range(NCH):
                            mm(bank.t[:, 0:T], slab.t[:, c, hf:hf + 128], hTb.t[:, c, :], c == 0, c == NCH - 1,
                               [slab.b, hTbb[c]], [bank.b])
                        r_ = rl[i % 2]
                        S.op("act", lambda: A.activation(out=r_.t[:, :], in_=bank.t[:, 0:T], func=AF.Relu),
                             rd=[bank.b], wr=[r_.b])
                        S.op("pool", lambda: G.tensor_tensor(out=u.t[:, i, :], in0=r_.t[:, :], in1=r_.t[:, :], op=ALU.mult),
                             rd=[r_.b], wr=[ub[i]])
                    for d in range(NCH):
                        if d % 2 == 0:
                            slab = next_slab()
                            S.dma("pool", slab.t[:, :, :], w_down_r[:, 16 * q4:16 * q4 + 16, d * 128:(d + 2) * 128],
                                  wr=[slab.b], ndesc=128)
                        hf = (d % 2) * 128
                        bank = nxt("pj", PJ)
                        for i in range(16):
                            mm(bank.t[:, 0:T], slab.t[:, i, hf:hf + 128], u.t[:, i, :], i == 0, i == 15, [slab.b, ub[i]], [bank.b])
                        S.op("dve", lambda: V.tensor_tensor(out=zs.t[:, d, :], in0=bank.t[:, 0:T], in1=zs.t[:, d, :],
                                                            op=ALU.add), rd=[bank.b, zsb[d]], wr=[zsb[d]])
                for d in range(NCH):
                    stats_accum(d)
                stats_final()
                for d in range(NCH):
                    normalize(d)
                    S.op("act", lambda: A.activation(out=zs.t[:, d, :], in_=zs.t[:, d, :], func=AF.Identity,
                                                     scale=lnp.t[:, 2, d:d + 1], bias=lnp.t[:, 3, d:d + 1]),
                         rd=[zsb[d], lnp.b], wr=[zsb[d]])
                oc = 0
                for t in range(ntt):
                    for g in range(4):
                        bank = nxt("tr", TR)
                        for i in range(4):
                            d = 4 * g + i
                            tp(bank.t[:, i * 128:(i + 1) * 128], zs.t[:, d, t * 128:(t + 1) * 128], [zsb[d]], [bank.b])
                        ystg = yst[oc % 2]
                        if oc % 2 == 0:
                            S.op("act", lambda: A.activation(out=ystg.t[:, :], in_=bank.t[:, :], func=AF.Copy),
                                 rd=[bank.b], wr=[ystg.b])
                        else:
                            S.op("dve", lambda: V.tensor_copy(out=ystg.t[:, :], in_=bank.t[:, :]), rd=[bank.b], wr=[ystg.b])
                        oc += 1
                        if sample:
                            S.dma("sp", ys[:, g * 512:(g + 1) * 512], ystg.t[:, :], rd=[ystg.b])
                        else:
                            r0 = (step * NQT + t) * 128
                            S.dma("sp", y[r0:r0 + 128, g * 512:(g + 1) * 512], ystg.t[:, :], rd=[ystg.b])
                S.barrier()

        steps = list(range(NSTEP + 1)) if DEBUG_PARTS is None else DEBUG_PARTS
        for st in (steps if STOP_AT > 0 else []):
            emit_step(st)
        S.finish()
    return nc


def _tables(core):
    j = core % 4
    cs = j * 2048
    inv = (1.0 / (np.float32(10000.0) ** (np.arange(32, dtype=np.float32) / np.float32(32)))).astype(np.float32)
    c2 = np.zeros((128, 33, 64), np.float32)
    s2 = np.zeros((128, 33, 64), np.float32)
    p = np.arange(128)
    for tw in range(33):
        if tw < 32:
            pos = (cs - 2048 + tw * 128 + p).astype(np.float32)
        else:
            pos = (PAST + (p % 8)).astype(np.float32)
        ang = (pos[:, None] * inv[None, :]).astype(np.float32)
        co = np.cos(ang).astype(np.float32)
        si = np.sin(ang).astype(np.float32)
        c2[:, tw, 0:32] = co
        c2[:, tw, 32:64] = co
        s2[:, tw, 0:32] = -si
        s2[:, tw, 32:64] = si
    valid = np.ones((128, 4 * NWT + 1), np.float32)
    for s in range(NSTEP):
        for wt in range(NWT):
            gpos = cs - 2048 + (4 * s + wt) * 128
            if gpos < 0:
                valid[:, s * NWT + wt] = 0.0
    return c2.reshape(128, -1), s2.reshape(128, -1), valid


def _masks():
    k = np.arange(128)[:, None]
    q = np.arange(128)[None, :]
    dl = q - k
    m4 = (dl % 4 == 0).astype(np.float32)
    m16 = (dl % 16 == 0).astype(np.float32)
    le0 = (dl <= 0).astype(np.float32)
    ge0 = (dl >= 0).astype(np.float32)
    mk = np.zeros((128, 8, 128), np.float32)
    mk[:, 0] = ge0
    mk[:, 1] = (dl < 0).astype(np.float32)
    mk[:, 2] = ge0 * (1.0 + m4 + m16)
    mk[:, 3] = le0 + m4 + m16
    mk[:, 4] = m4 + m16
    mk[:, 5] = le0 * m4 + m16
    mk[:, 6] = m16
    mk[:, 7] = le0 * m16
    t = np.arange(8)[None, :]
    i = np.arange(128)[:, None]
    msb = np.zeros((128, 16, 8), np.float32)
    for kt in range(16):
        dist = 2048 + t - (128 * kt + i)
        msb[:, kt] = ((dist <= 128).astype(np.float32) + ((dist % 4 == 0) & (dist <= 512)).astype(np.float32)
                      + ((dist % 16 == 0) & (dist <= 2048)).astype(np.float32))
    mnb = np.zeros((128, 16, 8), np.float32)
    mna = np.zeros((128, 16, 8), np.float32)
    for sq in range(16):
        for tk in range(8):
            for tq in range(8):
                if tk <= tq:
                    d = tq - tk
                    mnb[sq * 8 + tk, sq, tq] = 1.0 + (d % 4 == 0) + (d % 16 == 0)
                    mna[sq * 8 + tk, sq, tq] = 1.0
    dist = 128 + t - i
    msa = ((dist >= 0) & (dist < 128)).astype(np.float32)
    return mk.reshape(128, -1), msb.reshape(128, -1), mnb.reshape(128, -1), msa, mna.reshape(128, -1)


_NC_CACHE = {}


def kernel(x_prompt, x_sample, cache_a_k, cache_a_v, cache_b_k, cache_b_v,
           w_in, sinks, w_out, ln1_g, ln1_b, w_up, w_down, ln2_g, ln2_b):
    f = lambda a: np.ascontiguousarray(np.asarray(a, dtype=np.float32))
    x_prompt, x_sample = f(x_prompt), f(x_sample)
    cache_a_k, cache_a_v, cache_b_k, cache_b_v = f(cache_a_k), f(cache_a_v), f(cache_b_k), f(cache_b_v)
    w_in_, w_out_, w_up_, w_down_ = f(w_in)[0], f(w_out)[0], f(w_up)[0], f(w_down)[0]
    sinks_ = f(sinks).reshape(1, 16)
    lnp = np.stack([f(ln1_g)[0], f(ln1_b)[0], f(ln2_g)[0], f(ln2_b)[0]], 0)
    lnp = np.ascontiguousarray(lnp.reshape(4, 16, 128).transpose(2, 0, 1)).reshape(128, 64)
    mk, msb, mnb, msa, mna = _masks()
    identf = np.eye(128, dtype=np.float32)
    alphai = (np.eye(128) * ALPHA).astype(np.float32)

    if "nc" not in _NC_CACHE:
        _NC_CACHE["nc"] = build_nc()
    nc = _NC_CACHE["nc"]

    full = DEBUG_PARTS is None or NSTEP in DEBUG_PARTS
    nsq_d = NSEQ if full else 1
    in_maps = []
    for c in range(8):
        b, j = c // 4, c % 4
        cs = j * 2048
        xh = np.zeros((4096, D), np.float32)
        lo = cs - 2048
        if lo < 0:
            xh[2048:] = x_prompt[b, 0:2048]
        else:
            xh[:] = x_prompt[b, lo:lo + 4096]
        sl = slice(c * NSEQ, (c + 1) * NSEQ)
        c2, s2, valid = _tables(c)
        in_maps.append({
            "xh": xh,
            "xs": np.ascontiguousarray(x_sample[sl].reshape(128, D)),
            "cak": np.ascontiguousarray(cache_a_k[0, sl].reshape(NSEQ, 128, 256)),
            "cav": np.ascontiguousarray(cache_a_v[0, sl].reshape(NSEQ, 128, 256)),
            "cbk": np.ascontiguousarray(cache_b_k[0, sl].reshape(NSEQ, 2048, 1024))[:nsq_d],
            "cbv": np.ascontiguousarray(cache_b_v[0, sl].reshape(NSEQ, 2048, 1024))[:nsq_d],
            "w_in": w_in_, "w_out": w_out_, "w_up": w_up_, "w_down": w_down_,
            "sinks": sinks_, "lnp": lnp, "c2": c2, "s2": s2, "valid": valid,
            "mk": mk, "msb": msb, "mnb": mnb, "msa": msa, "mna": mna,
            "identf": identf, "alphai": alphai,
        })
    res = run_bass_kernel_spmd(nc, in_maps, core_ids=list(range(8)))
    R = res.results
    y_prompt = np.zeros((2, 8192, D), np.float32)
    y_sample = np.zeros((128, 8, D), np.float32)
    pak = np.zeros((1, 2, 128, 4, 64), np.float32)
    pav = np.zeros_like(pak)
    pbk = np.zeros((1, 2, 2048, 16, 64), np.float32)
    pbv = np.zeros_like(pbk)
    sak = np.zeros((1, 128, 128, 4, 64), np.float32)
    sav = np.zeros_like(sak)
    sbk = np.zeros((1, 128, 2048, 16, 64), np.float32)
    sbv = np.zeros_like(sbk)
    for c in range(8):
        b, j = c // 4, c % 4
        r = R[c]
        y_prompt[b, j * 2048:(j + 1) * 2048] = r["y"]
        sl = slice(c * NSEQ, (c + 1) * NSEQ)
        y_sample[sl] = r["ys"].reshape(NSEQ, 8, D)
        if j == 3:
            pak[0, b] = r["pak"].reshape(128, 4, 64)
            pav[0, b] = r["pav"].reshape(128, 4, 64)
            pbk[0, b] = r["pbk"].reshape(2048, 16, 64)
            pbv[0, b] = r["pbv"].reshape(2048, 16, 64)
        sak[0, sl] = r["sak"].reshape(NSEQ, 128, 4, 64)
        sav[0, sl] = r["sav"].reshape(NSEQ, 128, 4, 64)
        if full:
            sbk[0, sl] = r["sbk"].reshape(NSEQ, 2048, 16, 64)
            sbv[0, sl] = r["sbv"].reshape(NSEQ, 2048, 16, 64)
    return (y_prompt, y_sample, pak, pav, pbk, pbv, sak, sav, sbk, sbv)
```
